# Optimizing a Trainium2 kernel written in Bass

```python
import math
import jax, jax.numpy as jnp
from jax import lax
import numpy as np

D_MODEL = 1024
BATCH = 16
SEQ = 4096
DEPTH = 1
DEC_BATCH = 16
DEC_SEQ = 64
PAST_LEN = 4096

CHUNK = 64
Q_BLOCK = 128
N_META = 16
EPS = 1e-6
A_HEADS = 4
A_DH = 64
A_DV = 2 * A_DH
A_ROT = A_DH // 4
A_THETA = 500000.0
R_HEADS = 4
R_DK = 64
R_DV = 128
R_THETA = 10000.0
A_Q = A_HEADS * 2 * A_DH
A_K = A_HEADS * 2 * A_DH
A_V = A_HEADS * A_DV
R_Q = R_HEADS * R_DK
R_K = R_HEADS * R_DK
R_V = R_HEADS * R_DV
R_G = R_HEADS * R_DV
P_IN = A_Q + A_K + A_V + R_Q + R_K + R_V + R_G
SPLITS = (A_Q, A_Q + A_K, A_Q + A_K + A_V, A_Q + A_K + A_V + R_Q,
          A_Q + A_K + A_V + R_Q + R_K, A_Q + A_K + A_V + R_Q + R_K + R_V)
MIX = A_HEADS * A_DV + R_HEADS * R_DV
D_FF = ((-(-8 * D_MODEL // 3) + 255) // 256) * 256

kernel_name = "hymba_diffattn_retention_stream_step"


def rms_norm(x, g=None):
    xf = x.astype(jnp.float32)
    y = xf * lax.rsqrt(jnp.mean(xf * xf, axis=-1, keepdims=True) + EPS)
    if g is not None:
        y = y * g.astype(jnp.float32)
    return y.astype(x.dtype)


def rope(x, pos, rot_dim, theta):
    half = rot_dim // 2
    inv = theta ** (-jnp.arange(half, dtype=jnp.float32) / half)
    ang = pos.astype(jnp.float32)[:, None] * inv[None, :]
    ang = ang.reshape((pos.shape[0],) + (1,) * (x.ndim - 3) + (half,))
    cos, sin = jnp.cos(ang), jnp.sin(ang)
    xr = x[..., :rot_dim].astype(jnp.float32)
    x1, x2 = xr[..., :half], xr[..., half:]
    rot = jnp.concatenate([x1 * cos - x2 * sin, x2 * cos + x1 * sin], axis=-1).astype(x.dtype)
    return jnp.concatenate([rot, x[..., rot_dim:]], axis=-1)


def mix_inputs(x, pos, lw):
    B, T = x.shape[:2]
    z = rms_norm(x, lw['g_mix']) @ lw['w_in']
    aq, ak, av, rq, rk, rv, rg = jnp.split(z, SPLITS, axis=-1)
    q = rope(rms_norm(aq.reshape(B, T, A_HEADS, 2, A_DH), lw['g_q']), pos, A_ROT, A_THETA)
    k = rope(rms_norm(ak.reshape(B, T, A_HEADS, 2, A_DH), lw['g_k']), pos, A_ROT, A_THETA)
    q = q.reshape(B, T, A_HEADS, 2 * A_DH)
    k = k.reshape(B, T, A_HEADS, 2 * A_DH)
    v = av.reshape(B, T, A_HEADS, A_DV)
    rq = rope(rq.reshape(B, T, R_HEADS, R_DK), pos, R_DK, R_THETA)
    rk = rope(rk.reshape(B, T, R_HEADS, R_DK), pos, R_DK, R_THETA) * (R_DK ** -0.5)
    rv = rv.reshape(B, T, R_HEADS, R_DV)
    return (q, k, v), (rq, rk, rv, rg)


def diff_lambda(lw, lam_init):
    f = lambda a: a.astype(jnp.float32)
    return (jnp.exp(jnp.sum(f(lw['lam_q1']) * f(lw['lam_k1'])))
            - jnp.exp(jnp.sum(f(lw['lam_q2']) * f(lw['lam_k2']))) + lam_init)


def diff_attend(q, k, v, mask, lam):
    B, T = q.shape[:2]
    L = k.shape[1]
    qc = q.reshape(B, T, A_HEADS, 2, A_DH)
    kc = k.reshape(B, L, A_HEADS, 2, A_DH)
    s = jnp.einsum('bthcd,blhcd->bchtl', qc, kc, preferred_element_type=jnp.float32) * (A_DH ** -0.5)
    if mask is not None:
        s = jnp.where(mask, s, -1e30)
    p = jax.nn.softmax(s, axis=-1)
    a = p[:, 0] - lam * p[:, 1]
    return jnp.einsum('bhtl,blhv->bthv', a.astype(v.dtype), v)


def retention_log_decay():
    return jnp.log(1.0 - 2.0 ** (-5.0 - jnp.arange(R_HEADS, dtype=jnp.float32)))


def retention_state(k, v, S, log_g):
    C = k.shape[1]
    idx = jnp.arange(C, dtype=jnp.float32)
    w = jnp.exp((C - 1 - idx)[:, None] * log_g[None, :])
    return (jnp.exp(C * log_g)[None, :, None, None] * S
            + jnp.einsum('bchk,ch,bchv->bhkv', k, w, v))


def retention_block(q, k, v, S, log_g):
    C = q.shape[1]
    idx = jnp.arange(C, dtype=jnp.float32)
    diff = idx[:, None] - idx[None, :]
    D = jnp.where(diff[..., None] >= 0, jnp.exp(jnp.maximum(diff, 0.0)[..., None] * log_g), 0.0)
    s = jnp.einsum('bnhk,bmhk->bnmh', q, k) * D[None]
    o_in = jnp.einsum('bnmh,bmhv->bnhv', s, v)
    cross = jnp.exp((idx + 1.0)[:, None] * log_g[None, :])
    o_x = jnp.einsum('bnhk,nh,bhkv->bnhv', q, cross, S)
    return o_in + o_x, retention_state(k, v, S, log_g)


def mix_output(x, a_out, r_out, r_gate, lw, lam_init):
    B, T = x.shape[:2]
    a = (rms_norm(a_out, lw['g_sub']) * (1.0 - lam_init)).reshape(B, T, A_HEADS * A_DV)
    r = jax.nn.silu(r_gate) * rms_norm(r_out).reshape(B, T, R_HEADS * R_DV).astype(r_gate.dtype)
    h = x + jnp.concatenate([a.astype(x.dtype), r.astype(x.dtype)], axis=-1) @ lw['w_out']
    hn = rms_norm(h, lw['g_ffn'])
    return h + (jax.nn.silu(hn @ lw['w_gate']) * (hn @ lw['w_up'])) @ lw['w_down']


def prompt_layer(hm, h, lw, lam, lam_init, need_meta):
    B, S = h.shape[:2]
    pos_m = jnp.arange(N_META, dtype=jnp.int32)
    pos_f = N_META + jnp.arange(S, dtype=jnp.int32)
    (aqm, akm, avm), (rqm, rkm, rvm, rgm) = mix_inputs(hm, pos_m, lw)
    (aq, ak, av), (rq, rk, rv, rg) = mix_inputs(h, pos_f, lw)
    k_all = jnp.concatenate([jnp.broadcast_to(akm, (B,) + akm.shape[1:]), ak], axis=1)
    v_all = jnp.concatenate([jnp.broadcast_to(avm, (B,) + avm.shape[1:]), av], axis=1)
    key_chunk = jnp.concatenate([jnp.full((N_META,), -1, jnp.int32), jnp.arange(S, dtype=jnp.int32) // CHUNK])
    nb = S // Q_BLOCK
    q_blocks = jnp.moveaxis(aq.reshape(B, nb, Q_BLOCK, A_HEADS, 2 * A_DH), 1, 0)

    def attend_block(args):
        qb, b = args
        q_chunk = (b * Q_BLOCK + jnp.arange(Q_BLOCK, dtype=jnp.int32)) // CHUNK
        mask = key_chunk[None, :] <= q_chunk[:, None]
        return diff_attend(qb, k_all, v_all, mask, lam)

    ao = lax.map(attend_block, (q_blocks, jnp.arange(nb, dtype=jnp.int32)))
    ao = jnp.moveaxis(ao, 0, 1).reshape(B, S, A_HEADS, A_DV)
    log_g = retention_log_decay()
    s0 = jnp.zeros((1, R_HEADS, R_DK, R_DV), jnp.float32)
    if need_meta:
        rom, s_meta = retention_block(rqm, rkm, rvm, s0, log_g)
    else:
        s_meta = retention_state(rkm, rvm, s0, log_g)
    nc = S // CHUNK

    def to_chunks(t):
        return jnp.moveaxis(t.reshape((B, nc, CHUNK) + t.shape[2:]), 1, 0)

    def step(state, qkv):
        o, state = retention_block(*qkv, state, log_g)
        return state, o

    s_fin, ro = lax.scan(step, jnp.broadcast_to(s_meta, (B,) + s_meta.shape[1:]),
                         (to_chunks(rq), to_chunks(rk), to_chunks(rv)))
    ro = jnp.moveaxis(ro, 0, 1).reshape(B, S, R_HEADS, R_DV)
    h = mix_output(h, ao, ro, rg, lw, lam_init)
    if need_meta:
        aom = diff_attend(aqm, akm, avm, None, lam)
        hm = mix_output(hm, aom, rom, rgm, lw, lam_init)
    return hm, h, k_all, v_all, s_fin


def sample_layer(h, ck, cv, S, lw, lam, lam_init):
    B, T = h.shape[:2]
    pos = N_META + PAST_LEN + jnp.arange(T, dtype=jnp.int32)
    (aq, ak, av), (rq, rk, rv, rg) = mix_inputs(h, pos, lw)
    k_all = jnp.concatenate([ck, ak.astype(ck.dtype)], axis=1)
    v_all = jnp.concatenate([cv, av.astype(cv.dtype)], axis=1)
    ao = diff_attend(aq, k_all, v_all, None, lam)
    ro, s_new = retention_block(rq, rk, rv, S, retention_log_decay())
    h = mix_output(h, ao, ro, rg, lw, lam_init)
    return h, ak, av, s_new


def setup_inputs(seed: int = 0) -> dict:
    key = jax.random.key(seed)
    ks = jax.random.split(key, 24)
    nrm = lambda k, shape, s: jax.random.normal(k, shape, jnp.float32) * s
    return {
        'x_prompt': nrm(ks[0], (BATCH, SEQ, D_MODEL), 1.0),
        'x_sample': nrm(ks[1], (DEC_BATCH, DEC_SEQ, D_MODEL), 1.0),
        'cache_k': nrm(ks[2], (DEPTH, DEC_BATCH, N_META + PAST_LEN, A_HEADS, 2 * A_DH), 1.0),
        'cache_v': nrm(ks[3], (DEPTH, DEC_BATCH, N_META + PAST_LEN, A_HEADS, A_DV), 1.0),
        'state_ret': nrm(ks[4], (DEPTH, DEC_BATCH, R_HEADS, R_DK, R_DV), 0.5),
        'meta': nrm(ks[5], (N_META, D_MODEL), 1.0),
        'g_mix': 1.0 + nrm(ks[6], (DEPTH, D_MODEL), 0.1),
        'w_in': nrm(ks[7], (DEPTH, D_MODEL, P_IN), D_MODEL ** -0.5),
        'g_q': 1.0 + nrm(ks[8], (DEPTH, A_DH), 0.1),
        'g_k': 1.0 + nrm(ks[9], (DEPTH, A_DH), 0.1),
        'lam_q1': nrm(ks[10], (DEPTH, A_DH), 0.1),
        'lam_k1': nrm(ks[11], (DEPTH, A_DH), 0.1),
        'lam_q2': nrm(ks[12], (DEPTH, A_DH), 0.1),
        'lam_k2': nrm(ks[13], (DEPTH, A_DH), 0.1),
        'g_sub': 1.0 + nrm(ks[14], (DEPTH, A_DV), 0.1),
        'w_out': nrm(ks[15], (DEPTH, MIX, D_MODEL), MIX ** -0.5),
        'g_ffn': 1.0 + nrm(ks[16], (DEPTH, D_MODEL), 0.1),
        'w_gate': nrm(ks[17], (DEPTH, D_MODEL, D_FF), D_MODEL ** -0.5),
        'w_up': nrm(ks[18], (DEPTH, D_MODEL, D_FF), D_MODEL ** -0.5),
        'w_down': nrm(ks[19], (DEPTH, D_FF, D_MODEL), D_FF ** -0.5),
    }


def reference(x_prompt, x_sample, cache_k, cache_v, state_ret, meta, g_mix, w_in, g_q, g_k,
              lam_q1, lam_k1, lam_q2, lam_k2, g_sub, w_out, g_ffn, w_gate, w_up, w_down):
    hm = meta[None].astype(x_prompt.dtype)
    hp = x_prompt
    hs = x_sample
    kp, vp, sp, ksn, vsn, ssn = [], [], [], [], [], []
    for l in range(DEPTH):
        lw = dict(g_mix=g_mix[l], w_in=w_in[l], g_q=g_q[l], g_k=g_k[l],
                  lam_q1=lam_q1[l], lam_k1=lam_k1[l], lam_q2=lam_q2[l], lam_k2=lam_k2[l],
                  g_sub=g_sub[l], w_out=w_out[l], g_ffn=g_ffn[l],
                  w_gate=w_gate[l], w_up=w_up[l], w_down=w_down[l])
        lam_init = 0.8 - 0.6 * math.exp(-0.3 * l)
        lam = diff_lambda(lw, lam_init)
        hm, hp, k_l, v_l, s_l = prompt_layer(hm, hp, lw, lam, lam_init, l < DEPTH - 1)
        hs, ks_l, vs_l, ss_l = sample_layer(hs, cache_k[l], cache_v[l], state_ret[l], lw, lam, lam_init)
        kp.append(k_l); vp.append(v_l); sp.append(s_l)
        ksn.append(ks_l); vsn.append(vs_l); ssn.append(ss_l)
    return (hp, hs, jnp.stack(kp), jnp.stack(vp), jnp.stack(sp), jnp.stack(ksn), jnp.stack(vsn), jnp.stack(ssn))
```

```python
import math
import contextlib
import numpy as np
import ml_dtypes
import concourse.bass as bass
import concourse.mybir as mybir
from concourse.bass_utils import run_bass_kernel_spmd

F32 = mybir.dt.float32
BF16 = mybir.dt.bfloat16
AF = mybir.ActivationFunctionType
ALU = mybir.AluOpType
AX = mybir.AxisListType

NCORES = 8
D = 1024
SEQ = 4096
NM = 16
DEC = 64
PAST = 4096
LCACHE = NM + PAST
DFF = 2816
EPS = 1e-6
NFC = DFF // 128
NDMA = 40
NSW = 8
DEBUG_STOP = None
ENGS = ("pe", "act", "dve", "pool", "sp")


class Buf:
    __slots__ = ("name", "w", "r", "rd", "excl")

    def __init__(self, name="", excl=False):
        self.name = name
        self.excl = excl
        self.w = None
        self.r = {}
        self.rd = []


class Ins:
    __slots__ = ("eng", "fn", "deps", "sig", "val", "sem", "dma")


class Prog:
    def __init__(self):
        self.q = {e: [] for e in ENGS}
        self.ndma = 0
        self.dma_last = {}
        self.dma_cnt = {}
        self.bar = {}
        self.nd = {}

    def barrier(self):
        deps = []
        for e in ENGS:
            for ins in reversed(self.q[e]):
                if not ins.dma:
                    deps.append(ins)
                    break
        deps += list(self.dma_last.values())
        for e in ENGS:
            self.bar[e] = list(deps)

    def op(self, eng, fn, r=(), w=(), dma=False):
        ins = Ins()
        ins.eng = eng
        ins.fn = fn
        ins.dma = dma
        ins.sig = dma
        ins.val = 0
        deps = {}
        for b in r:
            if b.w is not None:
                deps[id(b.w)] = b.w
            if b.excl:
                for x in b.r.values():
                    if x.eng != eng:
                        deps[id(x)] = x
        for b in w:
            if b.w is not None:
                deps[id(b.w)] = b.w
            for x in b.r.values():
                deps[id(x)] = x
            for x in b.rd:
                deps[id(x)] = x
        if eng in self.bar:
            for x in self.bar.pop(eng):
                deps[id(x)] = x
        dl = []
        for d in deps.values():
            if d.eng == "pe" and eng == "pe" and not d.dma and not dma:
                continue
            d.sig = True
            dl.append(d)
        if dma:
            kind = "sw" if eng == "pool" else "dma"
            n = self.nd.get(kind, 0)
            self.nd[kind] = n + 1
            k = (kind, n % (NSW if kind == "sw" else NDMA))
            self.ndma += 1
            prev = self.dma_last.get(k)
            if prev is not None:
                dl.append(prev)
            self.dma_last[k] = ins
            c = self.dma_cnt.get(k, 0) + 1
            self.dma_cnt[k] = c
            ins.sem = k
            ins.val = 16 * c
        else:
            ins.sem = ("eng", eng)
        ins.deps = dl
        for b in r:
            if dma:
                b.rd.append(ins)
            else:
                b.r[eng] = ins
        for b in w:
            b.w = ins
            b.r = {}
            b.rd = []
        self.q[eng].append(ins)
        return ins

    def emit(self, nc):
        for e in ENGS:
            c = 0
            for ins in self.q[e]:
                if not ins.dma and ins.sig:
                    c += 1
                    ins.val = c
        with contextlib.ExitStack() as st:
            sems = {}
            for e in ("pe", "act", "dve", "pool"):
                sems[("eng", e)] = st.enter_context(nc.semaphore("s_" + e))
            for k in range(NDMA):
                sems[("dma", k)] = st.enter_context(nc.semaphore("d_%d" % k))
            for k in range(NSW):
                sems[("sw", k)] = st.enter_context(nc.semaphore("w_%d" % k))
            block = st.enter_context(nc.Block())
            finals = list(self.dma_last.values())

            def run(e, q, final=False):
                waited = {}
                for ins in q:
                    best = {}
                    for d in ins.deps:
                        if best.get(d.sem, 0) < d.val:
                            best[d.sem] = d.val
                    for s, v in best.items():
                        if waited.get(s, 0) < v:
                            e.wait_ge(sems[s], v)
                            waited[s] = v
                    bi = ins.fn(e)
                    if ins.sig:
                        bi.then_inc(sems[ins.sem], 16 if ins.dma else 1)
                if final:
                    for d in finals:
                        if waited.get(d.sem, 0) < d.val:
                            e.wait_ge(sems[d.sem], d.val)
                            waited[d.sem] = d.val

            @block.tensor
            def _(e):
                run(e, self.q["pe"])

            @block.scalar
            def _(e):
                run(e, self.q["act"])

            @block.vector
            def _(e):
                run(e, self.q["dve"])

            @block.gpsimd
            def _(e):
                run(e, self.q["pool"])

            @block.sync
            def _(e):
                run(e, self.q["sp"], final=True)


class _Stop(Exception):
    pass


class Ring:
    def __init__(self, nc, name, shape, dtype, n):
        self.t = [nc.alloc_sbuf_tensor("%s%d" % (name, i), shape, dtype) for i in range(n)]
        self.b = [Buf("%s%d" % (name, i)) for i in range(n)]
        self.n = n
        self.i = 0

    def next(self):
        k = self.i % self.n
        self.i += 1
        return self.t[k], self.b[k]


def _host_consts():
    f32 = np.float32
    pos = np.zeros((34, 128), np.int32)
    pos[0] = np.arange(128)
    for j in range(32):
        pos[1 + j] = NM + 128 * j + np.arange(128)
    pos[33] = NM + PAST + (np.arange(128) % DEC)
    posf = pos.astype(f32)
    invA = (f32(500000.0) ** (-np.arange(8, dtype=f32) / f32(8))).astype(f32)
    invR = (f32(10000.0) ** (-np.arange(32, dtype=f32) / f32(32))).astype(f32)
    angA = (posf[:, :, None] * invA[None, None, :]).astype(f32)
    angR = (posf[:, :, None] * invR[None, None, :]).astype(f32)
    rope = np.concatenate([np.cos(angA), np.sin(angA), np.cos(angR), np.sin(angR)], axis=-1).astype(f32)
    rope = np.ascontiguousarray(rope.transpose(1, 0, 2)).reshape(128, 34 * 80)
    g = (1.0 - 2.0 ** (-5.0 - np.arange(4))).astype(np.float64)
    lg = np.log(g)
    m = np.arange(128)
    dmaskP = np.zeros((128, 4, 128), np.float64)
    for h in range(4):
        dm = np.exp(-(m[:, None] + 1.0) * lg[h]) * (m[None, :] >= m[:, None])
        dmaskP[:, h, :] = dm
    crossP = np.exp((m[:, None] + 1.0) * lg[None, :])
    w8P = np.exp((127.0 - m[:, None]) * lg[None, :]) / 8.0
    mm = m % 64
    bb = m // 64
    dmaskS = np.zeros((128, 4, 128), np.float64)
    for h in range(4):
        dm = np.exp(-(mm[:, None] + 1.0) * lg[h]) * ((mm[None, :] >= mm[:, None]) & (bb[None, :] == bb[:, None]))
        dmaskS[:, h, :] = dm
    crossS = np.exp((mm[:, None] + 1.0) * lg[None, :])
    w8S = np.exp((63.0 - mm[:, None]) * lg[None, :]) / 8.0
    w8S0 = w8S * (bb[:, None] == 0)
    w8S1 = w8S * (bb[:, None] == 1)
    w8M = np.exp((15.0 - m[:, None]) * lg[None, :]) / 8.0 * (m[:, None] < 16)
    hp = m // 64
    acolP = np.stack([np.exp(128.0 * lg[2 * p + hp]) for p in range(2)], axis=1)
    acolS = np.stack([np.exp(64.0 * lg[2 * p + hp]) for p in range(2)], axis=1)
    small = np.concatenate([crossP, w8P, crossS, w8S0, w8S1, w8M, acolP, acolS, np.zeros((128, 2))], axis=1).astype(f32)
    dmask = np.concatenate([dmaskP.reshape(128, 512), dmaskS.reshape(128, 512)], axis=1).astype(f32)
    bf = ml_dtypes.bfloat16
    ident = np.eye(128)
    ones128 = np.ones((128, 128))
    ones16 = (m[:, None] < 16) * np.ones((128, 128))
    ones80 = (m[:, None] < 80) * np.ones((128, 128))
    maskL = np.zeros((128, 128)); maskL[0, 64:] = 1.0; maskL[64, 64:] = 1.0
    maskR = np.zeros((128, 64)); maskR[0, :] = -30000.0; maskR[64, :] = -30000.0
    cbf = np.concatenate([ident, ones128, ones16, ones80, maskL, maskR], axis=1).astype(bf)
    return dict(rope=rope, small=small, dmask=dmask, cbf=cbf)


def build_nc():
    nc = bass.Bass("TRN2", target_bir_lowering=False)
    P = Prog()
    O = P.op

    def din(name, shape, dt=F32):
        return nc.dram_tensor(name, list(shape), dt, kind="ExternalInput").ap()

    def dout(name, shape, dt=F32):
        return nc.dram_tensor(name, list(shape), dt, kind="ExternalOutput").ap()

    xp = din("xp", [2, SEQ, D]); xs = din("xs", [128, D])
    ck = din("ck", [2, LCACHE, 512]); cv = din("cv", [2, LCACHE, 512])
    st_in = din("st_in", [2, 4, 64, 128]); xmeta = din("xmeta", [128, D])
    w_in = din("w_in", [D, 3072]); w_out = din("w_out", [D, D])
    w_gate = din("w_gate", [D, DFF]); w_up = din("w_up", [D, DFF]); w_down = din("w_down", [DFF, D])
    gcols_d = din("gcols", [128, 16])
    gqk_d = din("gqk", [128, 128])
    gsub_d = din("gsub", [128, 1])
    lam4_d = din("lam4", [128, 256])
    rope_d = din("rope", [128, 34 * 80]); small_d = din("small", [128, 30])
    dmask_d = din("dmask", [128, 1024]); cbf_d = din("cbf", [128, 704], BF16)

    yp = dout("yp", [2, SEQ, D]); ys = dout("ys", [128, D])
    kp = dout("kp", [2, NM + SEQ, 512]); vp = dout("vp", [2, NM + SEQ, 512])
    stp = dout("stp", [2, 4, 64, 128]); ks = dout("ks", [128, 512]); vs = dout("vs", [128, 512])
    sts = dout("sts", [2, 4, 64, 128])

    Win_s = nc.dram_tensor("Win_s", [6, 128, 8, 512], BF16, kind="Internal").ap()
    Wout_s = nc.dram_tensor("Wout_s", [2, 128, 8, 512], BF16, kind="Internal").ap()
    Wgu_s = nc.dram_tensor("Wgu_s", [11, 128, 2, 8, 256], BF16, kind="Internal").ap()
    Wd_s = nc.dram_tensor("Wd_s", [DFF, D], BF16, kind="Internal").ap()
    bWin = [Buf() for _ in range(6)]; bWout = [Buf() for _ in range(2)]
    bWgu = [Buf() for _ in range(11)]; bWd = [Buf() for _ in range(6)]
    byp = {}; bks = Buf("ks"); bvs = Buf("vs")

    A = lambda name, shape, dt: nc.alloc_sbuf_tensor("sb_" + name, shape, dt)
    cbf = A("cbf", [128, 704], BF16); small = A("small", [128, 30], F32); dmask = A("dmask", [128, 512], F32)
    gcols = A("gcols", [128, 16], F32); gqk = A("gqk", [128, 128], F32); gsub = A("gsub", [128, 1], F32)
    gsubc = A("gsubc", [128, 1], F32); neglam = A("neglam", [128, 1], F32)
    bconst = Buf("const"); bdmask = Buf("dmask")
    ident = cbf[:, 0:128]; ones128 = cbf[:, 128:256]; ones16 = cbf[:, 256:384]; ones80 = cbf[:, 384:512]
    maskL = cbf[:, 512:640]; maskR = cbf[:, 640:704]
    crossP = small[:, 0:4]; w8P = small[:, 4:8]; crossS = small[:, 8:12]; w8S0 = small[:, 12:16]
    w8S1 = small[:, 16:20]; w8M = small[:, 20:24]; acolP = small[:, 24:26]; acolS = small[:, 26:28]; acol0 = small[:, 28:30]
    gq = gqk[:, 0:64]; gk = gqk[:, 64:128]

    kT = A("kT", [128, 4, 33 * 128], BF16); bkT = [Buf("kT%d" % i) for i in range(33)]
    v_sb = A("v_sb", [128, 33, 512], BF16); bv = [Buf("v%d" % i) for i in range(33)]
    stream = Ring(nc, "strm", [128, 4096], BF16, 4)
    xpool = Ring(nc, "xt", [128, D], F32, 3)
    xnb = Ring(nc, "xnb", [128, D], BF16, 2)
    nT = A("nT", [128, 8, 512], BF16); bnT = [Buf("nT%d" % i) for i in range(4)]
    qpad = A("qpad", [128, 4, 2, 512], BF16); bqpad = [Buf("qpad%d" % i) for i in range(4)]
    mixT = A("mixT", [128, 8, 512], BF16); bmix = [Buf("mix%d" % i) for i in range(8)]
    sgT = A("sgT", [128, 4, 512], BF16); bsg = [Buf("sg%d" % i) for i in range(4)]
    actT = A("actT", [128, NFC, 512], BF16); bact = [Buf("act%d" % i) for i in range(NFC)]
    Pb = Ring(nc, "Pb", [128, 2, 512], BF16, 3)
    T = Ring(nc, "T", [128, 512], F32, 6)
    Tb = Ring(nc, "Tb", [128, 512], BF16, 6)
    tiny = Ring(nc, "tiny", [128, 8], F32, 12)
    ropeT = Ring(nc, "ropeT", [128, 4, 80], F32, 2)
    qtpad = Ring(nc, "qtpad", [128, 4, 128], BF16, 2)
    rkT = Ring(nc, "rkT", [128, 2, 128], BF16, 2)
    kwb = Ring(nc, "kwb", [128, 256], BF16, 2)
    rvb = Ring(nc, "rvb", [128, 512], BF16, 2)
    ST = []
    for i_ in range(3):
        ST.append(dict(S=A("S_cur%d" % i_, [128, 2, 128], F32), bS=Buf("S%d" % i_),
                       Sbf=[A("S_bf%d_%d" % (i_, j_), [128, 2, 128], BF16) for j_ in range(3 if i_ == 0 else 2)],
                       bSb=[Buf("Sb%d_%d" % (i_, j_)) for j_ in range(3 if i_ == 0 else 2)], cur=0))

    pairs = [nc.alloc_psum_tensor("pair%d" % i, [128, 2, 512], F32) for i in range(4)]
    banks = [pairs[k // 2][:, k % 2, :] for k in range(8)]
    bbank = [Buf("bank%d" % i, excl=True) for i in range(8)]
    bk = {"cur": "ALL", "sets": {"ALL": list(range(8)), "R": [0, 1, 2, 3], "X": [4, 5, 6, 7]}, "i": {"ALL": 0, "R": 0, "X": 0}}

    def nbank():
        c = bk["cur"]
        s = bk["sets"][c]
        k = s[bk["i"][c] % len(s)]
        bk["i"][c] += 1
        return banks[k], bbank[k]

    def bc(ap, shape, axis):
        return ap.unsqueeze(axis).broadcast_to(shape)

    for dst, src in ((cbf, cbf_d), (small, small_d), (gcols, gcols_d), (gqk, gqk_d), (gsub, gsub_d)):
        O("sp", lambda e, dst=dst, src=src: e.dma_start(out=dst[:], in_=src), w=[bconst], dma=True)
    O("sp", lambda e: e.dma_start(out=dmask[:], in_=dmask_d[:, 0:512]), w=[bdmask], dma=True)

    win_v = w_in.rearrange("(kc p) (c n) -> c p kc n", p=128, n=512)
    for c in (3, 4, 1, 2, 5, 0):
        O("pool", lambda e, c=c: e.dma_start(out=Win_s[c], in_=win_v[c]), w=[bWin[c]], dma=True)
    wout_v = w_out.rearrange("(kc p) (c n) -> c p kc n", p=128, n=512)
    for c in range(2):
        O("pool", lambda e, c=c: e.dma_start(out=Wout_s[c], in_=wout_v[c]), w=[bWout[c]], dma=True)
    wg_v = w_gate.rearrange("(kc p) (c n) -> c p kc n", p=128, n=256)
    wu_v = w_up.rearrange("(kc p) (c n) -> c p kc n", p=128, n=256)
    for c in range(11):
        O("pool", lambda e, c=c: e.dma_start(out=Wgu_s[c, :, 0], in_=wg_v[c]), w=[bWgu[c]], dma=True)
        O("pool", lambda e, c=c: e.dma_start(out=Wgu_s[c, :, 1], in_=wu_v[c]), w=[bWgu[c]], dma=True)
    for c in range(6):
        r0 = c * 512
        r1 = min(DFF, r0 + 512)
        O("pool", lambda e, r0=r0, r1=r1: e.dma_start(out=Wd_s[r0:r1, :], in_=w_down[r0:r1, :]), w=[bWd[c]], dma=True)

    lam4, blam4 = T.next()
    O("sp", lambda e: e.dma_start(out=lam4[:, 0:256], in_=lam4_d), w=[blam4], dma=True)
    lt, blt = T.next()
    l3 = lam4[:, 0:256].rearrange("p (a b d) -> p a b d", a=2, b=2)
    O("dve", lambda e: e.tensor_tensor(out=lt[:, 0:128].rearrange("p (a d) -> p a d", a=2), in0=l3[:, :, 0, :], in1=l3[:, :, 1, :], op=ALU.mult), r=[blam4], w=[blt])
    s12, bs12 = tiny.next()
    O("dve", lambda e: e.tensor_reduce(out=s12[:, 0:2], in_=lt[:, 0:128].rearrange("p (a d) -> p a d", a=2), axis=AX.X, op=ALU.add), r=[blt], w=[bs12])
    O("act", lambda e: e.activation(out=s12[:, 0:2], in_=s12[:, 0:2], func=AF.Exp), r=[bs12], w=[bs12])
    O("dve", lambda e: e.tensor_tensor(out=neglam[:], in0=s12[:, 1:2], in1=s12[:, 0:1], op=ALU.subtract), r=[bs12], w=[bconst])
    O("dve", lambda e: e.tensor_scalar(out=neglam[:], in0=neglam[:], scalar1=-0.2, scalar2=None, op0=ALU.add), r=[bconst], w=[bconst])
    O("dve", lambda e: e.tensor_scalar(out=gsubc[:], in0=gsub[:], scalar1=0.8 * math.sqrt(128.0), scalar2=None, op0=ALU.mult), r=[bconst], w=[bconst])
    O("pool", lambda e: e.memset(qpad[:], 0.0), w=bqpad)
    for i in range(2):
        O("pool", lambda e, i=i: e.memset(qtpad.t[i][:], 0.0), w=[qtpad.b[i]])
    for st_ in ST:
        O("pool", lambda e, st_=st_: e.memset(st_["S"][:], 0.0), w=[st_["bS"]])
        for j_ in range(len(st_["Sbf"])):
            O("pool", lambda e, st_=st_, j_=j_: e.memset(st_["Sbf"][j_][:], 0.0), w=[st_["bSb"][j_]])

    def rstd_inplace(t, bt, n, scale, bias):
        O("act", lambda e: e.activation(out=t[:, 0:n], in_=t[:, 0:n], func=AF.Ln, scale=scale, bias=bias), r=[bt], w=[bt])
        O("act", lambda e: e.activation(out=t[:, 0:n], in_=t[:, 0:n], func=AF.Exp, scale=-0.5), r=[bt], w=[bt])

    def run_rr(gens, width):
        gens = list(gens)
        active = []
        while gens or active:
            while gens and len(active) < width:
                active.append(gens.pop(0))
            for g in list(active):
                try:
                    next(g)
                except StopIteration:
                    active.remove(g)

    def run_dag(items, width=4, cap=2):
        n = len(items)
        done = [False] * n
        started = [False] * n
        active = []
        while not all(done):
            for k in range(n):
                if len(active) >= width:
                    break
                it = items[k]
                if started[k] or not all(done[d] for d in it["deps"]):
                    continue
                if sum(1 for (j, g) in active if items[j]["typ"] == it["typ"]) >= (2 if it["typ"] in ("A", "R") else cap):
                    continue
                if any((not started[j]) and items[j]["typ"] == it["typ"] for j in range(k)):
                    continue
                started[k] = True
                active.append((k, it["gen"]()))
            assert active, "dag deadlock"
            for (k, g) in list(active):
                bk["cur"] = items[k].get("ring", "X")
                try:
                    next(g)
                except StopIteration:
                    done[k] = True
                    active.remove((k, g))
        bk["cur"] = "ALL"

    def norm_part1(xt, bx):
        xn, bxn = xnb.next()
        ss, bss = tiny.next()
        O("act", lambda e: e.activation(out=xn[:], in_=xt[:], func=AF.Square, accum_out=ss[:, 0:1]), r=[bx], w=[bxn, bss])
        rstd_inplace(ss, bss, 1, 1.0 / D, EPS)
        O("dve", lambda e: e.tensor_scalar(out=xn[:], in0=xt[:], scalar1=ss[:, 0:1], scalar2=None, op0=ALU.mult), r=[bx, bss], w=[bxn])
        return xn, bxn

    def norm_part2(xn, bxn, gcol, s):
        bank, bb = nbank()
        tpb = bank[:].bitcast(BF16)
        for kc in range(8):
            O("pe", lambda e, kc=kc: e.transpose(out=tpb[:, kc * 128:(kc + 1) * 128], in_=xn[:, kc * 128:(kc + 1) * 128], identity=ident), r=[bxn, bconst], w=[bb])
        O("dve", lambda e: e.tensor_tensor(out=nT[:, :, s * 128:(s + 1) * 128], in0=tpb.rearrange("p (k t) -> p k t", k=8),
                                           in1=bc(gcol, [128, 8, 128], 2), op=ALU.mult), r=[bb, bconst], w=[bnT[s]])

    def norm_transpose_gen(xt, bx, gcol, s):
        xn, bxn = norm_part1(xt, bx)
        yield
        norm_part2(xn, bxn, gcol, s)

    PRE = {}
    SCHED = {"list": [], "pos": 0}

    def x_source(kind, b, i, s):
        return {"meta": xmeta, "prompt": xp[b, i * 512 + s * 128:i * 512 + (s + 1) * 128, :] if kind == "prompt" else None, "sample": xs}[kind]

    PREW = {}

    def prefetch_weights_next():
        p = SCHED["pos"] + 1
        if p >= len(SCHED["list"]):
            return
        key = SCHED["list"][p]
        PREW[key] = dict(wg=load_stream(Win_s[5], bWin[5], v8x512), wq=load_stream(Win_s[3], bWin[3], v8x512),
                         wr=load_stream(Win_s[4], bWin[4], v8x512), w0=load_stream(Win_s[0], bWin[0], v8x512))

    def prefetch_next():
        p = SCHED["pos"] + 1
        if p >= len(SCHED["list"]):
            return
        kind, b, i = SCHED["list"][p]
        nsub = 4 if kind == "prompt" else 1
        for s in range(min(3, nsub)):
            xt, bx = xpool.next()
            src = x_source(kind, b, i, s)
            O("sp", lambda e, xt=xt, src=src: e.dma_start(out=xt[:], in_=src), w=[bx], dma=True)
            if s < 2:
                PRE[(kind, b, i, s)] = ("n",) + norm_part1(xt, bx)
            else:
                PRE[(kind, b, i, s)] = ("x", xt, bx)

    def rope(x3, bx, G, half, cos, sin, brope):
        ta, bta = T.next()
        tb, btb = T.next()
        n = G * half
        x4 = x3[:, :, 0:2 * half].rearrange("p g (two d) -> p g two d", two=2)
        u4 = ta[:, 0:2 * n].rearrange("p (g two d) -> p g two d", g=G, two=2)
        t4 = tb[:, 0:2 * n].rearrange("p (g two d) -> p g two d", g=G, two=2)
        cb4 = cos.unsqueeze(1).unsqueeze(1).broadcast_to([128, G, 2, half])
        sb = bc(sin, [128, G, half], 1)
        O("dve", lambda e: e.tensor_tensor(out=u4, in0=x4, in1=cb4, op=ALU.mult), r=[bx, brope], w=[bta])
        O("dve", lambda e: e.tensor_tensor(out=t4[:, :, 0, :], in0=x4[:, :, 1, :], in1=sb, op=ALU.mult), r=[bx, brope], w=[btb])
        O("dve", lambda e: e.tensor_tensor(out=t4[:, :, 1, :], in0=x4[:, :, 0, :], in1=sb, op=ALU.mult), r=[bx, brope], w=[btb])
        O("dve", lambda e: e.tensor_tensor(out=x4[:, :, 0, :], in0=u4[:, :, 0, :], in1=t4[:, :, 0, :], op=ALU.subtract), r=[bta, btb], w=[bx])
        O("dve", lambda e: e.tensor_tensor(out=x4[:, :, 1, :], in0=u4[:, :, 1, :], in1=t4[:, :, 1, :], op=ALU.add), r=[bta, btb], w=[bx])

    def load_stream(src_ap, bsrc, view):
        slot, bslot = stream.next()
        sv = view(slot)
        O("sp", lambda e: e.dma_start(out=sv, in_=src_ap), r=[bsrc], w=[bslot], dma=True)
        return sv, bslot

    v8x512 = lambda slot: slot[:, :].rearrange("p (k n) -> p k n", k=8)
    vgu = lambda slot: slot[:, :].rearrange("p (a k n) -> p a k n", a=2, k=8)

    def inproj_mm(wv, bw, s):
        bank, bb = nbank()
        for kc in range(8):
            O("pe", lambda e, kc=kc: e.matmul(bank[:, :], lhsT=nT[:, kc, s * 128:(s + 1) * 128], rhs=wv[:, kc, :], start=(kc == 0), stop=(kc == 7)),
              r=[bnT[s], bw], w=[bb])
        return bank, bb

    def qk_post(bank, bb, isq, rp, brp):
        sq, bsq = T.next()
        O("act", lambda e: e.activation(out=sq[:], in_=bank[:, :], func=AF.Square), r=[bb], w=[bsq])
        ssq, bssq = tiny.next()
        O("dve", lambda e: e.tensor_reduce(out=ssq[:, 0:8], in_=sq[:, :].rearrange("p (g d) -> p g d", g=8), axis=AX.X, op=ALU.add), r=[bsq], w=[bssq])
        rstd_inplace(ssq, bssq, 8, 1.0, 64 * EPS)
        qn, bqn = T.next()
        qn3 = qn[:, :].rearrange("p (g d) -> p g d", g=8)
        O("dve", lambda e: e.tensor_tensor(out=qn3, in0=bank[:, :].rearrange("p (g d) -> p g d", g=8), in1=bc(ssq[:, 0:8], [128, 8, 64], 2), op=ALU.mult),
          r=[bb, bssq], w=[bqn])
        if isq:
            O("dve", lambda e: e.tensor_tensor(out=qn3, in0=qn3, in1=bc(gq, [128, 8, 64], 1), op=ALU.mult), r=[bqn, bconst], w=[bqn])
        else:
            O("dve", lambda e: e.scalar_tensor_tensor(out=qn3, in0=qn3, scalar=8.0, in1=bc(gk, [128, 8, 64], 1), op0=ALU.mult, op1=ALU.mult), r=[bqn, bconst], w=[bqn])
        rope(qn3, bqn, 8, 8, rp[:, 0:8], rp[:, 8:16], brp)
        return qn, bqn

    def transpose4(src, bsrc):
        bank, bb = nbank()
        tp = bank[:].bitcast(BF16)
        for j in range(4):
            O("pe", lambda e, j=j: e.transpose(out=tp[:, j * 128:(j + 1) * 128], in_=src[:, j * 128:(j + 1) * 128], identity=ident), r=[bsrc, bconst], w=[bb])
        return tp[:, 0:512].rearrange("p (j t) -> p j t", j=4), bb

    def state_update(st_list, kw_list, rv, brv, acol):
        for st, (kw, bkw) in zip(st_list, kw_list):
            S, bSx = st["S"], st["bS"]
            bankU, bbU = nbank()
            for h in range(4):
                O("pe", lambda e, h=h, kw=kw, bankU=bankU: e.matmul(bankU[:, h * 128:(h + 1) * 128], lhsT=kw[:, (h // 2) * 128:(h // 2 + 1) * 128], rhs=rv[:, h * 128:(h + 1) * 128], start=True, stop=True), r=[bkw, brv], w=[bbU])
            for h in range(4):
                r0 = 64 * (h % 2)
                O("dve", lambda e, h=h, r0=r0, S=S, bankU=bankU: e.scalar_tensor_tensor(out=S[r0:r0 + 64, h // 2, :], in0=S[r0:r0 + 64, h // 2, :], scalar=acol[r0:r0 + 64, h // 2:h // 2 + 1],
                                                                                    in1=bankU[r0:r0 + 64, h * 128:(h + 1) * 128], op0=ALU.mult, op1=ALU.add), r=[bbU, bSx, bconst], w=[bSx])
            st["cur"] = (st["cur"] + 1) % len(st["Sbf"])
            Sbf, bSbx = st["Sbf"][st["cur"]], st["bSb"][st["cur"]]
            O("dve", lambda e, S=S, Sbf=Sbf: e.tensor_copy(out=Sbf[:], in_=S[:]), r=[bSx], w=[bSbx])

    def retention_out_gen(s, st_prev, tabs, qt, bqt, rk, brk, rv, brv):
        dm, cross, acol = tabs
        bankA, bbA = nbank()
        for h in range(4):
            O("pe", lambda e, h=h: e.matmul(bankA[:, h * 128:(h + 1) * 128], lhsT=rk[:, h // 2, :], rhs=qt[:, h, :], start=True, stop=True), r=[brk, bqt], w=[bbA])
        at, bat = Tb.next()
        O("dve", lambda e: e.tensor_tensor(out=at[:], in0=bankA[:, :], in1=dm, op=ALU.mult), r=[bbA, bdmask], w=[bat])
        yield
        bankO, bbO = nbank()
        for h in range(4):
            O("pe", lambda e, h=h: e.matmul(bankO[:, h * 128:(h + 1) * 128], lhsT=rv[:, h * 128:(h + 1) * 128], rhs=at[:, h * 128:(h + 1) * 128], start=True, stop=False), r=[brv, bat], w=[bbO])
            for (Sbf, bSbx, c0, c1) in st_prev:
                last = (c1 == 128)
                O("pe", lambda e, h=h, Sbf=Sbf, c0=c0, c1=c1, last=last: e.matmul(bankO[:, h * 128 + c0:h * 128 + c1], lhsT=Sbf[:, h // 2, :], rhs=qt[:, h, c0:c1], start=False, stop=last),
                  r=[bSbx, bqt], w=[bbO])
        osq, bosq = Tb.next()
        O("act", lambda e: e.activation(out=osq[:], in_=bankO[:, :], func=AF.Square), r=[bbO], w=[bosq])
        yield
        bankS, bbS = nbank()
        O("pe", lambda e: e.matmul(bankS[:, :], lhsT=ones128, rhs=osq[:], start=True, stop=True), r=[bosq, bconst], w=[bbS])
        rs, brs = T.next()
        O("act", lambda e: e.activation(out=rs[:], in_=bankS[:, :], func=AF.Ln, scale=1.0, bias=128 * EPS), r=[bbS], w=[brs])
        O("act", lambda e: e.activation(out=rs[:], in_=rs[:], func=AF.Exp, scale=-0.5), r=[brs], w=[brs])
        O("dve", lambda e: e.scalar_tensor_tensor(out=rs[:], in0=bankO[:, :], scalar=math.sqrt(128.0), in1=rs[:], op0=ALU.mult, op1=ALU.mult), r=[bbO, brs], w=[brs])
        O("pool", lambda e: e.tensor_tensor(out=mixT[:, 4:8, s * 128:(s + 1) * 128], in0=rs[:, :].rearrange("p (h t) -> p h t", h=4), in1=sgT[:, :, s * 128:(s + 1) * 128], op=ALU.mult),
          r=[brs] + bsg, w=bmix[4:8])

    def attention(h, N, q0, blocks, out_c0, pend=None):
        accs = [(banks[4], bbank[4]), (banks[5], bbank[5])]
        zb, bzb = banks[6], bbank[6]
        scr, bscr = banks[7], bbank[7]
        ssum, bssum = T.t[0], T.b[0]
        ones_of = {128: ones128, 16: ones16, 80: ones80}
        nb = len(blocks)
        sb = {}
        pend = list(pend) if pend else []

        def qk(k):
            kt, bkt, va, bva, nv, lo, diag = blocks[k]
            pair = []
            for c in range(2):
                bank, bb = banks[2 * (k % 2) + c], bbank[2 * (k % 2) + c]
                O("pe", lambda e, c=c, bank=bank, lo=lo, kt=kt, diag=diag: e.matmul(bank[:, lo:N], lhsT=kt[64 * c:64 * c + 64, :], rhs=qpad[64 * c:64 * c + 64, h, c, q0 + lo:q0 + N], start=True, stop=not diag), r=[bkt] + bqpad, w=[bb])
                if diag:
                    O("pe", lambda e, c=c, bank=bank, lo=lo: e.matmul(bank[:, lo:lo + 64], lhsT=maskL[64 * c:64 * c + 64, :], rhs=maskR[64 * c:64 * c + 64, :], start=False, stop=True), r=[bconst], w=[bb])
                pair.append((bank, bb))
            sb[k] = pair

        def ex(k):
            kt, bkt, va, bva, nv, lo, diag = blocks[k]
            pt, bpt = Pb.next()
            pr = pairs[k % 2]
            O("act", lambda e, pr=pr, lo=lo, pt=pt: e.activation(out=pt[:, :, lo:N], in_=pr[:, :, lo:N], func=AF.Exp), r=[sb[k][0][1], sb[k][1][1]], w=[bpt])
            sb[k] = (pt, bpt)
            if k == 0:
                O("dve", lambda e: e.memset(ssum[:, 0:N], 0.0), w=[bssum])
            O("dve", lambda e, pt=pt, nv=nv, lo=lo: e.tensor_tensor(out=ssum[0:nv, lo:N], in0=ssum[0:nv, lo:N], in1=pt[0:nv, 1, lo:N], op=ALU.add), r=[bpt, bssum], w=[bssum])

        def pv(k):
            kt, bkt, va, bva, nv, lo, diag = blocks[k]
            pt, bpt = sb.pop(k)
            first = (k == 0); last = (k == nb - 1)
            for c in range(2):
                O("pe", lambda e, c=c, lo=lo, va=va, pt=pt, first=first, last=last: e.matmul(accs[c][0][:, lo:N], lhsT=va, rhs=pt[:, c, lo:N], start=first, stop=last), r=[bva, bpt], w=[accs[c][1]])
            on = ones_of[nv]
            O("pe", lambda e, lo=lo, on=on, pt=pt, first=first, last=last: e.matmul(zb[:, lo:N], lhsT=on, rhs=pt[:, 0, lo:N], start=first, stop=last), r=[bconst, bpt], w=[bzb])

        qk(0)
        for k in range(nb):
            if k + 1 < nb:
                qk(k + 1)
            ex(k)
            pv(k)
            if k >= 1 and pend:
                pend.pop(0)()
        while pend:
            pend.pop(0)()

        a0c, ba0c = T.t[1], T.b[1]
        a1c, ba1c = T.t[2], T.b[2]
        z0c, bz0c = T.t[3], T.b[3]
        rz0, brz0 = T.t[4], T.b[4]
        rz1, brz1 = T.t[5], T.b[5]
        O("act", lambda e: e.activation(out=a0c[:, 0:N], in_=accs[0][0][:, 0:N], func=AF.Copy), r=[accs[0][1]], w=[ba0c])
        O("act", lambda e: e.activation(out=a1c[:, 0:N], in_=accs[1][0][:, 0:N], func=AF.Copy), r=[accs[1][1]], w=[ba1c])
        O("act", lambda e: e.activation(out=z0c[:, 0:N], in_=zb[:, 0:N], func=AF.Ln), r=[bzb], w=[bz0c])
        sbf, bsbf = Tb.next()
        O("dve", lambda e: e.tensor_copy(out=sbf[:, 0:N], in_=ssum[:, 0:N]), r=[bssum], w=[bsbf])
        hold = {}

        def t1():
            O("pe", lambda e: e.matmul(scr[:, 0:N], lhsT=ones128, rhs=sbf[:, 0:N], start=True, stop=True), r=[bsbf, bconst], w=[bscr])
            O("act", lambda e: e.activation(out=rz0[:, 0:N], in_=z0c[:, 0:N], func=AF.Exp, scale=-1.0), r=[bz0c], w=[brz0])

        def t2():
            O("act", lambda e: e.activation(out=rz1[:, 0:N], in_=scr[:, 0:N], func=AF.Ln), r=[bscr], w=[brz1])
            O("act", lambda e: e.activation(out=rz1[:, 0:N], in_=rz1[:, 0:N], func=AF.Exp, scale=-1.0), r=[brz1], w=[brz1])
            O("dve", lambda e: e.tensor_tensor(out=rz0[:, 0:N], in0=a0c[:, 0:N], in1=rz0[:, 0:N], op=ALU.mult), r=[ba0c, brz0], w=[brz0])

        def t3():
            O("dve", lambda e: e.tensor_tensor(out=rz1[:, 0:N], in0=a1c[:, 0:N], in1=rz1[:, 0:N], op=ALU.mult), r=[ba1c, brz1], w=[brz1])
            O("dve", lambda e: e.scalar_tensor_tensor(out=rz0[:, 0:N], in0=rz1[:, 0:N], scalar=neglam[:, 0:1], in1=rz0[:, 0:N], op0=ALU.mult, op1=ALU.add), r=[brz0, brz1, bconst], w=[brz0])

        def t4():
            osq, bosq = Tb.next()
            O("act", lambda e: e.activation(out=osq[:, 0:N], in_=rz0[:, 0:N], func=AF.Square), r=[brz0], w=[bosq])
            hold["osq"] = (osq, bosq)

        def t5():
            osq, bosq = hold["osq"]
            O("pe", lambda e: e.matmul(scr[:, 0:N], lhsT=ones128, rhs=osq[:, 0:N], start=True, stop=True), r=[bosq, bconst], w=[bscr])

        def t6():
            O("act", lambda e: e.activation(out=rz1[:, 0:N], in_=scr[:, 0:N], func=AF.Ln, scale=1.0, bias=128 * EPS), r=[bscr], w=[brz1])
            O("act", lambda e: e.activation(out=rz1[:, 0:N], in_=rz1[:, 0:N], func=AF.Exp, scale=-0.5), r=[brz1], w=[brz1])

        def t7():
            O("dve", lambda e: e.scalar_tensor_tensor(out=mixT[:, h, out_c0:out_c0 + N], in0=rz0[:, 0:N], scalar=gsubc[:, 0:1], in1=rz1[:, 0:N], op0=ALU.mult, op1=ALU.mult),
              r=[brz0, brz1, bconst], w=[bmix[h]])
        return [t1, t2, t3, t4, t5, t6, t7]

    def out_ffn(nsub, x_src, y_dst, ykey, wouts, wgu_pre, fpre):
        N = nsub * 128

        hts = {}
        xns = {}

        def F_mm(s):
            if s in fpre:
                ht, bht = fpre[s]
            else:
                ht, bht = xpool.next()
                O("sp", lambda e: e.dma_start(out=ht[:], in_=x_src(s)), w=[bht], dma=True)
            for half in range(2):
                wv, bw = wouts[half]
                bank, bb = nbank()
                for c in range(8):
                    O("pe", lambda e, c=c, bank=bank, wv=wv: e.matmul(bank[:, :], lhsT=mixT[:, c, s * 128:(s + 1) * 128], rhs=wv[:, c, :], start=(c == 0), stop=(c == 7)), r=[bmix[c], bw], w=[bb])
                O("dve", lambda e, half=half, bank=bank: e.tensor_tensor(out=ht[:, half * 512:(half + 1) * 512], in0=bank[:, :], in1=ht[:, half * 512:(half + 1) * 512], op=ALU.add), r=[bb, bht], w=[bht])
            byp[(ykey, s)] = Buf()
            O("sp", lambda e: e.dma_start(out=y_dst(s), in_=ht[:]), r=[bht], w=[byp[(ykey, s)]], dma=True)
            hts[s] = (ht, bht)

        def F_p1(s):
            xns[s] = norm_part1(*hts[s])

        def F_p2(s):
            norm_part2(xns[s][0], xns[s][1], gcols[:, 8:16], s)

        if nsub == 4:
            for step in (("m", 0), ("m", 1), ("1", 0), ("1", 1), ("m", 2), ("2", 0), ("1", 2), ("m", 3), ("2", 1), ("1", 3), ("2", 2), ("2", 3)):
                {"m": F_mm, "1": F_p1, "2": F_p2}[step[0]](step[1])
        else:
            F_mm(0); F_p1(0); F_p2(0)
        for c in range(11):
            if c in wgu_pre:
                wv, bw = wgu_pre[c]
            else:
                wv, bw = load_stream(Wgu_s[c], bWgu[c], vgu)
            for j in range(2):
                fc = 2 * c + j
                bg, bbg = nbank()
                bu, bbu = nbank()
                for a, (bank, bb) in enumerate(((bg, bbg), (bu, bbu))):
                    for kc in range(8):
                        O("pe", lambda e, a=a, kc=kc, j=j, bank=bank, wv=wv: e.matmul(bank[:, 0:N], lhsT=wv[:, a, kc, j * 128:(j + 1) * 128], rhs=nT[:, kc, 0:N], start=(kc == 0), stop=(kc == 7)),
                          r=[bw] + bnT[0:nsub], w=[bb])
                sg, bsgt = T.next()
                O("act", lambda e, sg=sg, bg=bg: e.activation(out=sg[:, 0:N], in_=bg[:, 0:N], func=AF.Silu), r=[bbg], w=[bsgt])
                O("dve", lambda e, sg=sg, bu=bu, fc=fc: e.tensor_tensor(out=actT[:, fc, 0:N], in0=bu[:, 0:N], in1=sg[:, 0:N], op=ALU.mult), r=[bbu, bsgt], w=[bact[fc]])
        prefetch_next()
        accs = [(banks[j], bbank[j]) for j in range(2 * nsub)]
        for c in range(6):
            nf = 4 if c < 5 else 2
            wv, bw = load_stream(Wd_s[c * 512:c * 512 + nf * 128, :].rearrange("(f p) n -> p f n", p=128), bWd[c],
                                 lambda slot, nf=nf: slot[:, 0:nf * 1024].rearrange("p (f n) -> p f n", f=nf))
            for f in range(nf):
                fc = 4 * c + f
                for s in range(nsub):
                    for half in range(2):
                        bank, bb = accs[2 * s + half]
                        O("pe", lambda e, fc=fc, f=f, s=s, half=half, bank=bank, wv=wv: e.matmul(bank[:, :], lhsT=actT[:, fc, s * 128:(s + 1) * 128], rhs=wv[:, f, half * 512:(half + 1) * 512],
                                                                                            start=(fc == 0), stop=(fc == NFC - 1)), r=[bact[fc], bw], w=[bb])
        order = (2, 3, 0, 1) if nsub == 4 else (0,)
        hv = {}
        for s in order:
            ht = actT[:, 4 * s:4 * s + 4, :].rearrange("p a n -> p (a n)").bitcast(F32)
            bht = bact[4 * s:4 * s + 4]
            O("sp", lambda e, ht=ht, s=s: e.dma_start(out=ht, in_=y_dst(s)), r=[byp[(ykey, s)]], w=bht, dma=True)
            hv[s] = (ht, bht)
        prefetch_weights_next()
        for s in order:
            ht, bht = hv[s]
            for half in range(2):
                bank, bb = accs[2 * s + half]
                O("dve", lambda e, ht=ht, half=half, bank=bank: e.tensor_tensor(out=ht[:, half * 512:(half + 1) * 512], in0=bank[:, :], in1=ht[:, half * 512:(half + 1) * 512], op=ALU.add), r=[bb] + bht, w=bht)
            O("sp", lambda e, ht=ht, s=s: e.dma_start(out=y_dst(s), in_=ht), r=bht, w=[byp[(ykey, s)]], dma=True)

    def tile(kind, b=0, i=0):
        nsub = 4 if kind == "prompt" else 1
        N = nsub * 128
        rt, brt = ropeT.next()
        t0 = {"meta": 0, "prompt": 1 + 4 * i, "sample": 33}[kind]
        O("sp", lambda e: e.dma_start(out=rt[:, 0:nsub, :], in_=rope_d[:, t0 * 80:(t0 + nsub) * 80].rearrange("p (s c) -> p s c", s=nsub)), w=[brt], dma=True)
        W = {}
        pw = PREW.pop((kind, b, i), None)
        if pw is not None:
            wg, bwg = pw["wg"]; wq, bwq = pw["wq"]; wr, bwr = pw["wr"]; W[0] = pw["w0"]
        else:
            if kind != "meta":
                wg, bwg = load_stream(Win_s[5], bWin[5], v8x512)
            wq, bwq = load_stream(Win_s[3], bWin[3], v8x512)
            wr, bwr = load_stream(Win_s[4], bWin[4], v8x512)
            if kind != "meta":
                W[0] = load_stream(Win_s[0], bWin[0], v8x512)

        def gen_A(s):
            pre = PRE.pop((kind, b, i, s), None)
            if pre is not None and pre[0] == "n":
                norm_part2(pre[1], pre[2], gcols[:, 0:8], s)
                return
                yield
            if pre is not None:
                xt, bx = pre[1], pre[2]
            else:
                xt, bx = xpool.next()
                src = x_source(kind, b, i, s)
                O("sp", lambda e: e.dma_start(out=xt[:], in_=src), w=[bx], dma=True)
            yield from norm_transpose_gen(xt, bx, gcols[:, 0:8], s)

        def gen_G():
            gq_ = []
            for h in range(4):
                bank, bb = nbank()
                for kc in range(8):
                    O("pe", lambda e, h=h, kc=kc, bank=bank: e.matmul(bank[:, 0:N], lhsT=wg[:, kc, h * 128:(h + 1) * 128], rhs=nT[:, kc, 0:N], start=(kc == 0), stop=(kc == 7)), r=[bwg] + bnT[0:nsub], w=[bb])
                gq_.append((h, bank, bb))
            for (h, bank, bb) in gq_:
                O("act", lambda e, h=h, bank=bank: e.activation(out=sgT[:, h, 0:N], in_=bank[:, 0:N], func=AF.Silu), r=[bb], w=[bsg[h]])
            return
            yield
        if kind == "prompt":
            tabs = (dmask[:, :], crossP, acolP)
            sts_ = [(ST[0], 0, 128)]
            w8l = [w8P]
        elif kind == "sample":
            tabs = (dmask[:, :], crossS, acolS)
            sts_ = [(ST[0], 0, 64), (ST[1], 64, 128)]
            w8l = [w8S0, w8S1]
        else:
            tabs = (dmask[:, :], crossP, acol0)
            sts_ = [(ST[2], 0, 128)]
            w8l = [w8M]

        RS = {}

        def gen_R1(s):
            rp = rt[:, s, :]
            bankq, bbq = inproj_mm(wq, bwq, s)
            bankr, bbr = inproj_mm(wr, bwr, s)
            rqk, brqk = T.next()
            O("act", lambda e: e.activation(out=rqk[:], in_=bankq[:, :], func=AF.Copy), r=[bbq], w=[brqk])
            rqk3 = rqk[:, :].rearrange("p (g d) -> p g d", g=8)
            rv, brv = rvb.next()
            O("act", lambda e: e.activation(out=rv[:], in_=bankr[:, :], func=AF.Copy), r=[bbr], w=[brv])
            rope(rqk3, brqk, 8, 32, rp[:, 16:48], rp[:, 48:80], brt)
            qtb, bqtb = Tb.next()
            O("dve", lambda e: e.tensor_tensor(out=qtb[:, 0:256].rearrange("p (h d) -> p h d", h=4), in0=rqk3[:, 0:4, :], in1=bc(tabs[1], [128, 4, 64], 2), op=ALU.mult), r=[brqk, bconst], w=[bqtb])
            O("dve", lambda e: e.tensor_scalar(out=qtb[:, 256:512], in0=rqk[:, 256:512], scalar1=0.125, scalar2=None, op0=ALU.mult), r=[brqk], w=[bqtb])
            kws = []
            for w8 in w8l:
                kw, bkw = kwb.next()
                O("dve", lambda e, kw=kw, w8=w8: e.tensor_tensor(out=kw[:, :].rearrange("p (h d) -> p h d", h=4), in0=rqk3[:, 4:8, :], in1=bc(w8, [128, 4, 64], 2), op=ALU.mult), r=[brqk, bconst], w=[bkw])
                kws.append((kw, bkw))
            yield
            tp3, bbt = transpose4(qtb, bqtb)
            qt, bqt = qtpad.next()
            qt3 = qt[:, :, :].rearrange("p (a two) t -> p a two t", two=2)
            O("dve", lambda e: e.tensor_copy(out=qt3[0:64, :, 0, :], in_=tp3[0:64, 0:2, :]), r=[bbt], w=[bqt])
            O("act", lambda e: e.activation(out=qt3[64:128, :, 1, :], in_=tp3[64:128, 0:2, :], func=AF.Copy), r=[bbt], w=[bqt])
            rk, brk = rkT.next()
            O("dve", lambda e: e.tensor_copy(out=rk[:], in_=tp3[:, 2:4, :]), r=[bbt], w=[brk])
            st_prev = [(st["Sbf"][st["cur"]], st["bSb"][st["cur"]], c0, c1) for (st, c0, c1) in sts_]
            state_update([st for (st, c0, c1) in sts_], kws, rv, brv, tabs[2])
            RS[s] = (st_prev, qt, bqt, rk, brk, rv, brv)

        def gen_R2(s):
            st_prev, qt, bqt, rk, brk, rv, brv = RS[s]
            yield from retention_out_gen(s, st_prev, tabs, qt, bqt, rk, brk, rv, brv)

        def gen_QKV(c, s, wv, bw):
            rp = rt[:, s, :]
            bank, bb = inproj_mm(wv, bw, s)
            tok0 = i * 512 + s * 128
            blk = 1 + 4 * i + s
            if c == 0:
                qn, bqn = qk_post(bank, bb, True, rp, brt)
                qb, bqb = Tb.next()
                O("act", lambda e: e.activation(out=qb[:], in_=qn[:], func=AF.Copy), r=[bqn], w=[bqb])
                yield
                tp3, bbt = transpose4(qb, bqb)
                O("dve", lambda e: e.tensor_copy(out=qpad[0:64, :, 0, s * 128:(s + 1) * 128], in_=tp3[0:64, :, :]), r=[bbt], w=[bqpad[s]])
                O("act", lambda e: e.activation(out=qpad[64:128, :, 1, s * 128:(s + 1) * 128], in_=tp3[64:128, :, :], func=AF.Copy), r=[bbt], w=[bqpad[s]])
            elif c == 1:
                kn, bkn = qk_post(bank, bb, False, rp, brt)
                if kind == "prompt":
                    O("sp", lambda e: e.dma_start(out=kp[b, NM + tok0:NM + tok0 + 128, :], in_=kn[:]), r=[bkn], dma=True)
                elif kind == "meta":
                    for bq in range(2):
                        O("sp", lambda e, bq=bq: e.dma_start(out=kp[bq, 0:NM, :], in_=kn[0:NM, :]), r=[bkn], dma=True)
                else:
                    O("sp", lambda e: e.dma_start(out=ks, in_=kn[:]), r=[bkn], w=[bks], dma=True)
                if kind != "sample":
                    kb, bkb = Tb.next()
                    O("act", lambda e: e.activation(out=kb[:], in_=kn[:], func=AF.Copy), r=[bkn], w=[bkb])
                    yield
                    tp3, bbt = transpose4(kb, bkb)
                    kblk = 0 if kind == "meta" else blk
                    O("dve", lambda e: e.tensor_copy(out=kT[:, :, kblk * 128:(kblk + 1) * 128], in_=tp3), r=[bbt], w=[bkT[kblk]])
            else:
                vf, bvf = T.next()
                O("act", lambda e: e.activation(out=vf[:], in_=bank[:, :], func=AF.Copy), r=[bb], w=[bvf])
                if kind == "prompt":
                    O("sp", lambda e: e.dma_start(out=vp[b, NM + tok0:NM + tok0 + 128, :], in_=vf[:]), r=[bvf], dma=True)
                elif kind == "meta":
                    for bq in range(2):
                        O("sp", lambda e, bq=bq: e.dma_start(out=vp[bq, 0:NM, :], in_=vf[0:NM, :]), r=[bvf], dma=True)
                else:
                    O("sp", lambda e: e.dma_start(out=vs, in_=vf[:]), r=[bvf], w=[bvs], dma=True)
                if kind != "sample":
                    vblk = 0 if kind == "meta" else blk
                    O("pool", lambda e: e.tensor_copy(out=v_sb[:, vblk, :], in_=vf[:]), r=[bvf], w=[bv[vblk]])

        def gen_load(c):
            W[c] = load_stream(Win_s[c], bWin[c], v8x512)
            return
            yield

        items = []
        idx = {}

        def add(name, gen, deps, typ, ring="X"):
            idx[name] = len(items)
            items.append(dict(gen=gen, deps=[idx[d] for d in deps], typ=typ, ring=ring))

        for s in range(nsub):
            add(("A", s), (lambda s=s: gen_A(s)), [], "A")
        if kind != "meta":
            add("G", gen_G, [("A", s) for s in range(nsub)], "G")
        for s in range(min(2, nsub)):
            add(("R1", s), (lambda s=s: gen_R1(s)), [("A", s)], "R", "R")
        for s in range(nsub):
            if kind != "meta":
                add(("R2", s), (lambda s=s: gen_R2(s)), [("R1", s), "G"], "R", "R")
            if s + 2 < nsub:
                add(("R1", s + 2), (lambda s=s: gen_R1(s + 2)), [("A", s + 2)] + ([("R2", s)] if kind != "meta" else []), "R", "R")
        if kind != "meta":
            for s in range(nsub):
                add(("Q", s), (lambda s=s: gen_QKV(0, s, *W[0])), [("A", s)], "Q")
        add("LK", (lambda: gen_load(1)), (["G"] if kind != "meta" else [("R1", s) for s in range(nsub)]), "L")
        for s in range(nsub):
            add(("K", s), (lambda s=s: gen_QKV(1, s, *W[1])), [("A", s), "LK"], "K")
        add("LV", (lambda: gen_load(2)), [("R1", s) for s in range(nsub)] + ["LK"], "L")
        for s in range(nsub):
            add(("V", s), (lambda s=s: gen_QKV(2, s, *W[2])), [("A", s), "LV"], "V")
        run_dag(items, width=6, cap=3)
        if kind == "meta":
            return
        wouts = [load_stream(Wout_s[half], bWout[half], v8x512) for half in range(2)]
        wgu_pre = {c: load_stream(Wgu_s[c], bWgu[c], vgu) for c in range(2)}
        fpre = {}
        for s_ in range(min(3, nsub)):
            xt_, bx_ = xpool.next()
            O("sp", lambda e, xt_=xt_, s_=s_: e.dma_start(out=xt_[:], in_=x_source(kind, b, i, s_)), w=[bx_], dma=True)
            fpre[s_] = (xt_, bx_)
        if kind == "prompt":
            tail = []
            for h in range(4):
                blocks = [(kT[:, h, 0:128], bkT[0], v_sb[:, 0, h * 128:(h + 1) * 128], bv[0], 16, 0, False)]
                for j in range(4 * i):
                    blocks.append((kT[:, h, (1 + j) * 128:(2 + j) * 128], bkT[1 + j], v_sb[:, 1 + j, h * 128:(h + 1) * 128], bv[1 + j], 128, 0, False))
                for jj in range(4):
                    j = 4 * i + jj
                    blocks.append((kT[:, h, (1 + j) * 128:(2 + j) * 128], bkT[1 + j], v_sb[:, 1 + j, h * 128:(h + 1) * 128], bv[1 + j], 128, 128 * jj, True))
                tail = attention(h, 512, 0, blocks, 0, tail)
            for f_ in tail:
                f_()
            out_ffn(4, lambda s: xp[b, i * 512 + s * 128:i * 512 + (s + 1) * 128, :], lambda s: yp[b, i * 512 + s * 128:i * 512 + (s + 1) * 128, :], ("p", b, i), wouts, wgu_pre, fpre)
        else:
            def sample_stream(sbi, tail):
                for f_ in tail:
                    f_()
                tail = []
                for blk in range(33):
                    kf, bkf = T.next()
                    vf, bvf = T.next()
                    if blk < 32:
                        O("sp", lambda e, kf=kf, blk=blk: e.dma_start(out=kf[:], in_=ck[sbi, blk * 128:(blk + 1) * 128, :]), w=[bkf], dma=True)
                        O("sp", lambda e, vf=vf, blk=blk: e.dma_start(out=vf[:], in_=cv[sbi, blk * 128:(blk + 1) * 128, :]), w=[bvf], dma=True)
                    else:
                        O("pool", lambda e, kf=kf: e.memset(kf[:], 0.0), w=[bkf])
                        O("pool", lambda e, vf=vf: e.memset(vf[:], 0.0), w=[bvf])
                        O("sp", lambda e, kf=kf: e.dma_start(out=kf[0:NM, :], in_=ck[sbi, PAST:LCACHE, :]), w=[bkf], dma=True)
                        O("sp", lambda e, vf=vf: e.dma_start(out=vf[0:NM, :], in_=cv[sbi, PAST:LCACHE, :]), w=[bvf], dma=True)
                        O("sp", lambda e, kf=kf: e.dma_start(out=kf[NM:NM + DEC, :], in_=ks[sbi * DEC:(sbi + 1) * DEC, :]), r=[bks], w=[bkf], dma=True)
                        O("sp", lambda e, vf=vf: e.dma_start(out=vf[NM:NM + DEC, :], in_=vs[sbi * DEC:(sbi + 1) * DEC, :]), r=[bvs], w=[bvf], dma=True)
                    kb, bkb = Tb.next()
                    O("pool", lambda e, kb=kb, kf=kf: e.tensor_copy(out=kb[:], in_=kf[:]), r=[bkf], w=[bkb])
                    tp3, bbt = transpose4(kb, bkb)
                    O("dve", lambda e, tp3=tp3, blk=blk: e.tensor_copy(out=kT[:, :, blk * 128:(blk + 1) * 128], in_=tp3), r=[bbt], w=[bkT[blk]])
                    O("act", lambda e, vf=vf, blk=blk: e.activation(out=v_sb[:, blk, :], in_=vf[:], func=AF.Copy), r=[bvf], w=[bv[blk]])
                for h in range(4):
                    blocks = []
                    for j in range(33):
                        blocks.append((kT[:, h, j * 128:(j + 1) * 128], bkT[j], v_sb[:, j, h * 128:(h + 1) * 128], bv[j], 128 if j < 32 else 80, 0, False))
                    tail = attention(h, DEC, sbi * DEC, blocks, sbi * DEC, tail)
                return tail
            tail = []
            for sbi_ in range(2):
                tail = sample_stream(sbi_, tail)
            for f_ in tail:
                f_()
            out_ffn(1, lambda s: xs, lambda s: ys, ("s", 0, 0), wouts, wgu_pre, fpre)

    if DEBUG_STOP == "pro":
        P.emit(nc)
        return nc
    SCHED["list"] = [("meta", 0, 0)] + [("prompt", b_, i_) for b_ in range(2) for i_ in range(SEQ // 512)] + [("sample", 0, 0)]
    try:
        tile("meta")
    except _Stop:
        P.emit(nc)
        return nc
    if DEBUG_STOP == "meta":
        P.emit(nc)
        return nc
    for b in range(2):
        O("dve", lambda e: e.tensor_copy(out=ST[0]["S"][:], in_=ST[2]["S"][:]), r=[ST[2]["bS"]], w=[ST[0]["bS"]])
        O("pool", lambda e, c_=ST[0]["cur"]: e.tensor_copy(out=ST[0]["Sbf"][c_][:], in_=ST[2]["S"][:]), r=[ST[2]["bS"]], w=[ST[0]["bSb"][ST[0]["cur"]]])
        for i in range(SEQ // 512):
            if isinstance(DEBUG_STOP, int) and b * 8 + i >= DEBUG_STOP:
                P.emit(nc)
                return nc
            SCHED["pos"] = 1 + b * (SEQ // 512) + i
            tile("prompt", b, i)
        for pr in range(2):
            O("sp", lambda e, pr=pr, b=b: e.dma_start(out=stp[b, 2 * pr:2 * pr + 2].rearrange("h k v -> (h k) v"), in_=ST[0]["S"][:, pr, :]), r=[ST[0]["bS"]], dma=True)
    O("sp", lambda e: e.dma_start(out=dmask[:], in_=dmask_d[:, 512:1024]), w=[bdmask], dma=True)
    for sbi in range(2):
        for pr in range(2):
            O("sp", lambda e, pr=pr, sbi=sbi: e.dma_start(out=ST[sbi]["S"][:, pr, :], in_=st_in[sbi, 2 * pr:2 * pr + 2].rearrange("h k v -> (h k) v")), w=[ST[sbi]["bS"]], dma=True)
        O("pool", lambda e, sbi=sbi, c_=ST[sbi]["cur"]: e.tensor_copy(out=ST[sbi]["Sbf"][c_][:], in_=ST[sbi]["S"][:]), r=[ST[sbi]["bS"]], w=[ST[sbi]["bSb"][ST[sbi]["cur"]]])
    SCHED["pos"] = len(SCHED["list"]) - 1
    tile("sample")
    for sbi in range(2):
        for pr in range(2):
            O("sp", lambda e, pr=pr, sbi=sbi: e.dma_start(out=sts[sbi, 2 * pr:2 * pr + 2].rearrange("h k v -> (h k) v"), in_=ST[sbi]["S"][:, pr, :]), r=[ST[sbi]["bS"]], dma=True)
    P.emit(nc)
    return nc


_CACHE = {}


def kernel(x_prompt, x_sample, cache_k, cache_v, state_ret, meta, g_mix, w_in, g_q, g_k,
           lam_q1, lam_k1, lam_q2, lam_k2, g_sub, w_out, g_ffn, w_gate, w_up, w_down):
    f32 = np.float32
    A = lambda a: np.ascontiguousarray(np.asarray(a, dtype=f32))
    if "nc" not in _CACHE:
        _CACHE["nc"] = build_nc()
        _CACHE["consts"] = _host_consts()
    nc = _CACHE["nc"]
    cst = _CACHE["consts"]
    x_prompt = A(x_prompt); x_sample = A(x_sample); cache_k = A(cache_k); cache_v = A(cache_v); state_ret = A(state_ret)
    xmeta = np.zeros((128, D), f32); xmeta[:NM] = A(meta)
    gcols = np.concatenate([A(g_mix)[0].reshape(8, 128).T, A(g_ffn)[0].reshape(8, 128).T], axis=1)
    gqk = np.concatenate([np.broadcast_to(A(g_q)[0][None, :], (128, 64)), np.broadcast_to(A(g_k)[0][None, :], (128, 64))], axis=1)
    gsub = A(g_sub)[0].reshape(128, 1)
    lam4 = np.concatenate([A(lam_q1)[0], A(lam_k1)[0], A(lam_q2)[0], A(lam_k2)[0]])[None, :]
    lam4 = np.broadcast_to(lam4, (128, 256))
    shared = dict(xmeta=xmeta, w_in=A(w_in)[0], w_out=A(w_out)[0], w_gate=A(w_gate)[0], w_up=A(w_up)[0], w_down=A(w_down)[0],
                  gcols=A(gcols), gqk=A(gqk), gsub=A(gsub), lam4=A(lam4),
                  rope=cst["rope"], small=cst["small"], dmask=cst["dmask"], cbf=cst["cbf"])
    in_maps = []
    for c in range(NCORES):
        m = dict(shared)
        m["xp"] = x_prompt[2 * c:2 * c + 2]
        m["xs"] = x_sample[2 * c:2 * c + 2].reshape(128, D)
        m["ck"] = cache_k[0, 2 * c:2 * c + 2].reshape(2, LCACHE, 512)
        m["cv"] = cache_v[0, 2 * c:2 * c + 2].reshape(2, LCACHE, 512)
        m["st_in"] = state_ret[0, 2 * c:2 * c + 2]
        in_maps.append(m)
    res = run_bass_kernel_spmd(nc, in_maps, core_ids=list(range(NCORES)))
    R = res.results
    cat = lambda k: np.concatenate([np.asarray(r[k]) for r in R], axis=0)
    y_prompt = cat("yp").reshape(16, SEQ, D)
    y_sample = np.concatenate([np.asarray(r["ys"]).reshape(2, DEC, D) for r in R], axis=0)
    k_prompt = cat("kp").reshape(1, 16, NM + SEQ, 4, 128)
    v_prompt = cat("vp").reshape(1, 16, NM + SEQ, 4, 128)
    state_prompt = cat("stp").reshape(1, 16, 4, 64, 128)
    k_sample = np.concatenate([np.asarray(r["ks"]).reshape(2, DEC, 4, 128) for r in R], axis=0)[None]
    v_sample = np.concatenate([np.asarray(r["vs"]).reshape(2, DEC, 4, 128) for r in R], axis=0)[None]
    state_sample = cat("sts").reshape(1, 16, 4, 64, 128)
    return tuple(np.ascontiguousarray(a, dtype=f32) for a in
                 (y_prompt, y_sample, k_prompt, v_prompt, state_prompt, k_sample, v_sample, state_sample))
```

```python
import math
import contextlib
import numpy as np
import ml_dtypes
import concourse.bass as bass
import concourse.mybir as mybir
from concourse.bass_utils import run_bass_kernel_spmd

F32 = mybir.dt.float32
BF16 = mybir.dt.bfloat16
AF = mybir.ActivationFunctionType
ALU = mybir.AluOpType
AX = mybir.AxisListType

NCORES = 8
D = 1024
SEQ = 4096
NM = 16
DEC = 64
PAST = 4096
LCACHE = NM + PAST
DFF = 2816
EPS = 1e-6
NFC = DFF // 128
NDMA = 40
NSW = 8
DEBUG_STOP = None
ENGS = ("pe", "act", "dve", "pool", "sp")


class Buf:
    __slots__ = ("name", "w", "r", "rd", "excl")

    def __init__(self, name="", excl=False):
        self.name = name
        self.excl = excl
        self.w = None
        self.r = {}
        self.rd = []


class Ins:
    __slots__ = ("eng", "fn", "deps", "sig", "val", "sem", "dma")


class Prog:
    def __init__(self):
        self.q = {e: [] for e in ENGS}
        self.ndma = 0
        self.dma_last = {}
        self.dma_cnt = {}
        self.bar = {}
        self.nd = {}

    def barrier(self):
        deps = []
        for e in ENGS:
            for ins in reversed(self.q[e]):
                if not ins.dma:
                    deps.append(ins)
                    break
        deps += list(self.dma_last.values())
        for e in ENGS:
            self.bar[e] = list(deps)

    def op(self, eng, fn, r=(), w=(), dma=False):
        ins = Ins()
        ins.eng = eng
        ins.fn = fn
        ins.dma = dma
        ins.sig = dma
        ins.val = 0
        deps = {}
        for b in r:
            if b.w is not None:
                deps[id(b.w)] = b.w
            if b.excl:
                for x in b.r.values():
                    if x.eng != eng:
                        deps[id(x)] = x
        for b in w:
            if b.w is not None:
                deps[id(b.w)] = b.w
            for x in b.r.values():
                deps[id(x)] = x
            for x in b.rd:
                deps[id(x)] = x
        if eng in self.bar:
            for x in self.bar.pop(eng):
                deps[id(x)] = x
        dl = []
        for d in deps.values():
            if d.eng == "pe" and eng == "pe" and not d.dma and not dma:
                continue
            d.sig = True
            dl.append(d)
        if dma:
            kind = "sw" if eng == "pool" else "dma"
            n = self.nd.get(kind, 0)
            self.nd[kind] = n + 1
            k = (kind, n % (NSW if kind == "sw" else NDMA))
            self.ndma += 1
            prev = self.dma_last.get(k)
            if prev is not None:
                dl.append(prev)
            self.dma_last[k] = ins
            c = self.dma_cnt.get(k, 0) + 1
            self.dma_cnt[k] = c
            ins.sem = k
            ins.val = 16 * c
        else:
            ins.sem = ("eng", eng)
        ins.deps = dl
        for b in r:
            if dma:
                b.rd.append(ins)
            else:
                b.r[eng] = ins
        for b in w:
            b.w = ins
            b.r = {}
            b.rd = []
        self.q[eng].append(ins)
        return ins

    def emit(self, nc):
        for e in ENGS:
            c = 0
            for ins in self.q[e]:
                if not ins.dma and ins.sig:
                    c += 1
                    ins.val = c
        with contextlib.ExitStack() as st:
            sems = {}
            for e in ("pe", "act", "dve", "pool"):
                sems[("eng", e)] = st.enter_context(nc.semaphore("s_" + e))
            for k in range(NDMA):
                sems[("dma", k)] = st.enter_context(nc.semaphore("d_%d" % k))
            for k in range(NSW):
                sems[("sw", k)] = st.enter_context(nc.semaphore("w_%d" % k))
            block = st.enter_context(nc.Block())
            finals = list(self.dma_last.values())

            def run(e, q, final=False):
                waited = {}
                for ins in q:
                    best = {}
                    for d in ins.deps:
                        if best.get(d.sem, 0) < d.val:
                            best[d.sem] = d.val
                    for s, v in best.items():
                        if waited.get(s, 0) < v:
                            e.wait_ge(sems[s], v)
                            waited[s] = v
                    bi = ins.fn(e)
                    if ins.sig:
                        bi.then_inc(sems[ins.sem], 16 if ins.dma else 1)
                if final:
                    for d in finals:
                        if waited.get(d.sem, 0) < d.val:
                            e.wait_ge(sems[d.sem], d.val)
                            waited[d.sem] = d.val

            @block.tensor
            def _(e):
                run(e, self.q["pe"])

            @block.scalar
            def _(e):
                run(e, self.q["act"])

            @block.vector
            def _(e):
                run(e, self.q["dve"])

            @block.gpsimd
            def _(e):
                run(e, self.q["pool"])

            @block.sync
            def _(e):
                run(e, self.q["sp"], final=True)


class _Stop(Exception):
    pass


class Ring:
    def __init__(self, nc, name, shape, dtype, n):
        self.t = [nc.alloc_sbuf_tensor("%s%d" % (name, i), shape, dtype) for i in range(n)]
        self.b = [Buf("%s%d" % (name, i)) for i in range(n)]
        self.n = n
        self.i = 0

    def next(self):
        k = self.i % self.n
        self.i += 1
        return self.t[k], self.b[k]


def _host_consts():
    f32 = np.float32
    pos = np.zeros((34, 128), np.int32)
    pos[0] = np.arange(128)
    for j in range(32):
        pos[1 + j] = NM + 128 * j + np.arange(128)
    pos[33] = NM + PAST + (np.arange(128) % DEC)
    posf = pos.astype(f32)
    invA = (f32(500000.0) ** (-np.arange(8, dtype=f32) / f32(8))).astype(f32)
    invR = (f32(10000.0) ** (-np.arange(32, dtype=f32) / f32(32))).astype(f32)
    angA = (posf[:, :, None] * invA[None, None, :]).astype(f32)
    angR = (posf[:, :, None] * invR[None, None, :]).astype(f32)
    rope = np.concatenate([np.cos(angA), np.sin(angA), np.cos(angR), np.sin(angR)], axis=-1).astype(f32)
    rope = np.ascontiguousarray(rope.transpose(1, 0, 2)).reshape(128, 34 * 80)
    g = (1.0 - 2.0 ** (-5.0 - np.arange(4))).astype(np.float64)
    lg = np.log(g)
    m = np.arange(128)
    dmaskP = np.zeros((128, 4, 128), np.float64)
    for h in range(4):
        dm = np.exp(-(m[:, None] + 1.0) * lg[h]) * (m[None, :] >= m[:, None])
        dmaskP[:, h, :] = dm
    crossP = np.exp((m[:, None] + 1.0) * lg[None, :])
    w8P = np.exp((127.0 - m[:, None]) * lg[None, :]) / 8.0
    mm = m % 64
    bb = m // 64
    dmaskS = np.zeros((128, 4, 128), np.float64)
    for h in range(4):
        dm = np.exp(-(mm[:, None] + 1.0) * lg[h]) * ((mm[None, :] >= mm[:, None]) & (bb[None, :] == bb[:, None]))
        dmaskS[:, h, :] = dm
    crossS = np.exp((mm[:, None] + 1.0) * lg[None, :])
    w8S = np.exp((63.0 - mm[:, None]) * lg[None, :]) / 8.0
    w8S0 = w8S * (bb[:, None] == 0)
    w8S1 = w8S * (bb[:, None] == 1)
    w8M = np.exp((15.0 - m[:, None]) * lg[None, :]) / 8.0 * (m[:, None] < 16)
    hp = m // 64
    acolP = np.stack([np.exp(128.0 * lg[2 * p + hp]) for p in range(2)], axis=1)
    acolS = np.stack([np.exp(64.0 * lg[2 * p + hp]) for p in range(2)], axis=1)
    small = np.concatenate([crossP, w8P, crossS, w8S0, w8S1, w8M, acolP, acolS, np.zeros((128, 2))], axis=1).astype(f32)
    dmask = np.concatenate([dmaskP.reshape(128, 512), dmaskS.reshape(128, 512)], axis=1).astype(f32)
    bf = ml_dtypes.bfloat16
    ident = np.eye(128)
    ones128 = np.ones((128, 128))
    ones16 = (m[:, None] < 16) * np.ones((128, 128))
    ones80 = (m[:, None] < 80) * np.ones((128, 128))
    maskL = np.zeros((128, 128)); maskL[0, 64:] = 1.0; maskL[64, 64:] = 1.0
    maskR = np.zeros((128, 64)); maskR[0, :] = -30000.0; maskR[64, :] = -30000.0
    cbf = np.concatenate([ident, ones128, ones16, ones80, maskL, maskR], axis=1).astype(bf)
    return dict(rope=rope, small=small, dmask=dmask, cbf=cbf)


def build_nc():
    nc = bass.Bass("TRN2", target_bir_lowering=False)
    P = Prog()
    O = P.op

    def din(name, shape, dt=F32):
        return nc.dram_tensor(name, list(shape), dt, kind="ExternalInput").ap()

    def dout(name, shape, dt=F32):
        return nc.dram_tensor(name, list(shape), dt, kind="ExternalOutput").ap()

    xp = din("xp", [2, SEQ, D]); xs = din("xs", [128, D])
    ck = din("ck", [2, LCACHE, 512]); cv = din("cv", [2, LCACHE, 512])
    st_in = din("st_in", [2, 4, 64, 128]); xmeta = din("xmeta", [128, D])
    w_in = din("w_in", [D, 3072]); w_out = din("w_out", [D, D])
    w_gate = din("w_gate", [D, DFF]); w_up = din("w_up", [D, DFF]); w_down = din("w_down", [DFF, D])
    gcols_d = din("gcols", [128, 16])
    gqk_d = din("gqk", [128, 128])
    gsub_d = din("gsub", [128, 1])
    lam4_d = din("lam4", [128, 256])
    rope_d = din("rope", [128, 34 * 80]); small_d = din("small", [128, 30])
    dmask_d = din("dmask", [128, 1024]); cbf_d = din("cbf", [128, 704], BF16)

    yp = dout("yp", [2, SEQ, D]); ys = dout("ys", [128, D])
    kp = dout("kp", [2, NM + SEQ, 512]); vp = dout("vp", [2, NM + SEQ, 512])
    stp = dout("stp", [2, 4, 64, 128]); ks = dout("ks", [128, 512]); vs = dout("vs", [128, 512])
    sts = dout("sts", [2, 4, 64, 128])

    Win_s = nc.dram_tensor("Win_s", [6, 128, 8, 512], BF16, kind="Internal").ap()
    Wout_s = nc.dram_tensor("Wout_s", [2, 128, 8, 512], BF16, kind="Internal").ap()
    Wgu_s = nc.dram_tensor("Wgu_s", [11, 128, 2, 8, 256], BF16, kind="Internal").ap()
    Wd_s = nc.dram_tensor("Wd_s", [DFF, D], BF16, kind="Internal").ap()
    bWin = [Buf() for _ in range(6)]; bWout = [Buf() for _ in range(2)]
    bWgu = [Buf() for _ in range(11)]; bWd = [Buf() for _ in range(6)]
    byp = {}; bks = Buf("ks"); bvs = Buf("vs")

    A = lambda name, shape, dt: nc.alloc_sbuf_tensor("sb_" + name, shape, dt)
    cbf = A("cbf", [128, 704], BF16); small = A("small", [128, 30], F32); dmask = A("dmask", [128, 512], F32)
    gcols = A("gcols", [128, 16], F32); gqk = A("gqk", [128, 128], F32); gsub = A("gsub", [128, 1], F32)
    gsubc = A("gsubc", [128, 1], F32); neglam = A("neglam", [128, 1], F32)
    bconst = Buf("const"); bdmask = Buf("dmask")
    ident = cbf[:, 0:128]; ones128 = cbf[:, 128:256]; ones16 = cbf[:, 256:384]; ones80 = cbf[:, 384:512]
    maskL = cbf[:, 512:640]; maskR = cbf[:, 640:704]
    crossP = small[:, 0:4]; w8P = small[:, 4:8]; crossS = small[:, 8:12]; w8S0 = small[:, 12:16]
    w8S1 = small[:, 16:20]; w8M = small[:, 20:24]; acolP = small[:, 24:26]; acolS = small[:, 26:28]; acol0 = small[:, 28:30]
    gq = gqk[:, 0:64]; gk = gqk[:, 64:128]

    kT = A("kT", [128, 4, 33 * 128], BF16); bkT = [Buf("kT%d" % i) for i in range(33)]
    v_sb = A("v_sb", [128, 33, 512], BF16); bv = [Buf("v%d" % i) for i in range(33)]
    stream = Ring(nc, "strm", [128, 4096], BF16, 4)
    xpool = Ring(nc, "xt", [128, D], F32, 3)
    xnb = Ring(nc, "xnb", [128, D], BF16, 2)
    nT = A("nT", [128, 8, 512], BF16); bnT = [Buf("nT%d" % i) for i in range(4)]
    qpad = A("qpad", [128, 4, 2, 512], BF16); bqpad = [Buf("qpad%d" % i) for i in range(4)]
    mixT = A("mixT", [128, 8, 512], BF16); bmix = [Buf("mix%d" % i) for i in range(8)]
    sgT = A("sgT", [128, 4, 512], BF16); bsg = [Buf("sg%d" % i) for i in range(4)]
    actT = A("actT", [128, NFC, 512], BF16); bact = [Buf("act%d" % i) for i in range(NFC)]
    Pb = Ring(nc, "Pb", [128, 2, 512], BF16, 3)
    T = Ring(nc, "T", [128, 512], F32, 6)
    Tb = Ring(nc, "Tb", [128, 512], BF16, 6)
    tiny = Ring(nc, "tiny", [128, 8], F32, 12)
    ropeT = Ring(nc, "ropeT", [128, 4, 80], F32, 2)
    qtpad = Ring(nc, "qtpad", [128, 4, 128], BF16, 2)
    rkT = Ring(nc, "rkT", [128, 2, 128], BF16, 2)
    kwb = Ring(nc, "kwb", [128, 256], BF16, 2)
    rvb = Ring(nc, "rvb", [128, 512], BF16, 2)
    ST = []
    for i_ in range(3):
        ST.append(dict(S=A("S_cur%d" % i_, [128, 2, 128], F32), bS=Buf("S%d" % i_),
                       Sbf=[A("S_bf%d_%d" % (i_, j_), [128, 2, 128], BF16) for j_ in range(3 if i_ == 0 else 2)],
                       bSb=[Buf("Sb%d_%d" % (i_, j_)) for j_ in range(3 if i_ == 0 else 2)], cur=0))

    pairs = [nc.alloc_psum_tensor("pair%d" % i, [128, 2, 512], F32) for i in range(4)]
    banks = [pairs[k // 2][:, k % 2, :] for k in range(8)]
    bbank = [Buf("bank%d" % i, excl=True) for i in range(8)]
    bk = {"cur": "ALL", "sets": {"ALL": list(range(8)), "R": [0, 1, 2, 3], "X": [4, 5, 6, 7]}, "i": {"ALL": 0, "R": 0, "X": 0}}

    def nbank():
        c = bk["cur"]
        s = bk["sets"][c]
        k = s[bk["i"][c] % len(s)]
        bk["i"][c] += 1
        return banks[k], bbank[k]

    def bc(ap, shape, axis):
        return ap.unsqueeze(axis).broadcast_to(shape)

    for dst, src in ((cbf, cbf_d), (small, small_d), (gcols, gcols_d), (gqk, gqk_d), (gsub, gsub_d)):
        O("sp", lambda e, dst=dst, src=src: e.dma_start(out=dst[:], in_=src), w=[bconst], dma=True)
    O("sp", lambda e: e.dma_start(out=dmask[:], in_=dmask_d[:, 0:512]), w=[bdmask], dma=True)

    win_v = w_in.rearrange("(kc p) (c n) -> c p kc n", p=128, n=512)
    for c in (3, 4, 1, 2, 5, 0):
        O("pool", lambda e, c=c: e.dma_start(out=Win_s[c], in_=win_v[c]), w=[bWin[c]], dma=True)
    wout_v = w_out.rearrange("(kc p) (c n) -> c p kc n", p=128, n=512)
    for c in range(2):
        O("pool", lambda e, c=c: e.dma_start(out=Wout_s[c], in_=wout_v[c]), w=[bWout[c]], dma=True)
    wg_v = w_gate.rearrange("(kc p) (c n) -> c p kc n", p=128, n=256)
    wu_v = w_up.rearrange("(kc p) (c n) -> c p kc n", p=128, n=256)
    for c in range(11):
        O("pool", lambda e, c=c: e.dma_start(out=Wgu_s[c, :, 0], in_=wg_v[c]), w=[bWgu[c]], dma=True)
        O("pool", lambda e, c=c: e.dma_start(out=Wgu_s[c, :, 1], in_=wu_v[c]), w=[bWgu[c]], dma=True)
    for c in range(6):
        r0 = c * 512
        r1 = min(DFF, r0 + 512)
        O("pool", lambda e, r0=r0, r1=r1: e.dma_start(out=Wd_s[r0:r1, :], in_=w_down[r0:r1, :]), w=[bWd[c]], dma=True)

    lam4, blam4 = T.next()
    O("sp", lambda e: e.dma_start(out=lam4[:, 0:256], in_=lam4_d), w=[blam4], dma=True)
    lt, blt = T.next()
    l3 = lam4[:, 0:256].rearrange("p (a b d) -> p a b d", a=2, b=2)
    O("dve", lambda e: e.tensor_tensor(out=lt[:, 0:128].rearrange("p (a d) -> p a d", a=2), in0=l3[:, :, 0, :], in1=l3[:, :, 1, :], op=ALU.mult), r=[blam4], w=[blt])
    s12, bs12 = tiny.next()
    O("dve", lambda e: e.tensor_reduce(out=s12[:, 0:2], in_=lt[:, 0:128].rearrange("p (a d) -> p a d", a=2), axis=AX.X, op=ALU.add), r=[blt], w=[bs12])
    O("act", lambda e: e.activation(out=s12[:, 0:2], in_=s12[:, 0:2], func=AF.Exp), r=[bs12], w=[bs12])
    O("dve", lambda e: e.tensor_tensor(out=neglam[:], in0=s12[:, 1:2], in1=s12[:, 0:1], op=ALU.subtract), r=[bs12], w=[bconst])
    O("dve", lambda e: e.tensor_scalar(out=neglam[:], in0=neglam[:], scalar1=-0.2, scalar2=None, op0=ALU.add), r=[bconst], w=[bconst])
    O("dve", lambda e: e.tensor_scalar(out=gsubc[:], in0=gsub[:], scalar1=0.8 * math.sqrt(128.0), scalar2=None, op0=ALU.mult), r=[bconst], w=[bconst])
    O("pool", lambda e: e.memset(qpad[:], 0.0), w=bqpad)
    for i in range(2):
        O("pool", lambda e, i=i: e.memset(qtpad.t[i][:], 0.0), w=[qtpad.b[i]])
    for st_ in ST:
        O("pool", lambda e, st_=st_: e.memset(st_["S"][:], 0.0), w=[st_["bS"]])
        for j_ in range(len(st_["Sbf"])):
            O("pool", lambda e, st_=st_, j_=j_: e.memset(st_["Sbf"][j_][:], 0.0), w=[st_["bSb"][j_]])

    def rstd_inplace(t, bt, n, scale, bias):
        O("act", lambda e: e.activation(out=t[:, 0:n], in_=t[:, 0:n], func=AF.Ln, scale=scale, bias=bias), r=[bt], w=[bt])
        O("act", lambda e: e.activation(out=t[:, 0:n], in_=t[:, 0:n], func=AF.Exp, scale=-0.5), r=[bt], w=[bt])

    def run_rr(gens, width):
        gens = list(gens)
        active = []
        while gens or active:
            while gens and len(active) < width:
                active.append(gens.pop(0))
            for g in list(active):
                try:
                    next(g)
                except StopIteration:
                    active.remove(g)

    def run_dag(items, width=4, cap=2):
        n = len(items)
        done = [False] * n
        started = [False] * n
        active = []
        while not all(done):
            for k in range(n):
                if len(active) >= width:
                    break
                it = items[k]
                if started[k] or not all(done[d] for d in it["deps"]):
                    continue
                if sum(1 for (j, g) in active if items[j]["typ"] == it["typ"]) >= (2 if it["typ"] in ("A", "R") else cap):
                    continue
                if any((not started[j]) and items[j]["typ"] == it["typ"] for j in range(k)):
                    continue
                started[k] = True
                active.append((k, it["gen"]()))
            assert active, "dag deadlock"
            for (k, g) in list(active):
                bk["cur"] = items[k].get("ring", "X")
                try:
                    next(g)
                except StopIteration:
                    done[k] = True
                    active.remove((k, g))
        bk["cur"] = "ALL"

    def norm_part1(xt, bx):
        xn, bxn = xnb.next()
        ss, bss = tiny.next()
        O("act", lambda e: e.activation(out=xn[:], in_=xt[:], func=AF.Square, accum_out=ss[:, 0:1]), r=[bx], w=[bxn, bss])
        rstd_inplace(ss, bss, 1, 1.0 / D, EPS)
        O("dve", lambda e: e.tensor_scalar(out=xn[:], in0=xt[:], scalar1=ss[:, 0:1], scalar2=None, op0=ALU.mult), r=[bx, bss], w=[bxn])
        return xn, bxn

    def norm_part2(xn, bxn, gcol, s):
        bank, bb = nbank()
        tpb = bank[:].bitcast(BF16)
        for kc in range(8):
            O("pe", lambda e, kc=kc: e.transpose(out=tpb[:, kc * 128:(kc + 1) * 128], in_=xn[:, kc * 128:(kc + 1) * 128], identity=ident), r=[bxn, bconst], w=[bb])
        O("dve", lambda e: e.tensor_tensor(out=nT[:, :, s * 128:(s + 1) * 128], in0=tpb.rearrange("p (k t) -> p k t", k=8),
                                           in1=bc(gcol, [128, 8, 128], 2), op=ALU.mult), r=[bb, bconst], w=[bnT[s]])

    def norm_transpose_gen(xt, bx, gcol, s):
        xn, bxn = norm_part1(xt, bx)
        yield
        norm_part2(xn, bxn, gcol, s)

    PRE = {}
    SCHED = {"list": [], "pos": 0}

    def x_source(kind, b, i, s):
        return {"meta": xmeta, "prompt": xp[b, i * 512 + s * 128:i * 512 + (s + 1) * 128, :] if kind == "prompt" else None, "sample": xs}[kind]

    PREW = {}

    def prefetch_weights_next():
        p = SCHED["pos"] + 1
        if p >= len(SCHED["list"]):
            return
        key = SCHED["list"][p]
        PREW[key] = dict(wg=load_stream(Win_s[5], bWin[5], v8x512), wq=load_stream(Win_s[3], bWin[3], v8x512),
                         wr=load_stream(Win_s[4], bWin[4], v8x512), w0=load_stream(Win_s[0], bWin[0], v8x512))

    def prefetch_next():
        p = SCHED["pos"] + 1
        if p >= len(SCHED["list"]):
            return
        kind, b, i = SCHED["list"][p]
        nsub = 4 if kind == "prompt" else 1
        for s in range(min(3, nsub)):
            xt, bx = xpool.next()
            src = x_source(kind, b, i, s)
            O("sp", lambda e, xt=xt, src=src: e.dma_start(out=xt[:], in_=src), w=[bx], dma=True)
            if s < 2:
                PRE[(kind, b, i, s)] = ("n",) + norm_part1(xt, bx)
            else:
                PRE[(kind, b, i, s)] = ("x", xt, bx)

    def rope(x3, bx, G, half, cos, sin, brope):
        ta, bta = T.next()
        tb, btb = T.next()
        n = G * half
        x4 = x3[:, :, 0:2 * half].rearrange("p g (two d) -> p g two d", two=2)
        u4 = ta[:, 0:2 * n].rearrange("p (g two d) -> p g two d", g=G, two=2)
        t4 = tb[:, 0:2 * n].rearrange("p (g two d) -> p g two d", g=G, two=2)
        cb4 = cos.unsqueeze(1).unsqueeze(1).broadcast_to([128, G, 2, half])
        sb = bc(sin, [128, G, half], 1)
        O("dve", lambda e: e.tensor_tensor(out=u4, in0=x4, in1=cb4, op=ALU.mult), r=[bx, brope], w=[bta])
        O("dve", lambda e: e.tensor_tensor(out=t4[:, :, 0, :], in0=x4[:, :, 1, :], in1=sb, op=ALU.mult), r=[bx, brope], w=[btb])
        O("dve", lambda e: e.tensor_tensor(out=t4[:, :, 1, :], in0=x4[:, :, 0, :], in1=sb, op=ALU.mult), r=[bx, brope], w=[btb])
        O("dve", lambda e: e.tensor_tensor(out=x4[:, :, 0, :], in0=u4[:, :, 0, :], in1=t4[:, :, 0, :], op=ALU.subtract), r=[bta, btb], w=[bx])
        O("dve", lambda e: e.tensor_tensor(out=x4[:, :, 1, :], in0=u4[:, :, 1, :], in1=t4[:, :, 1, :], op=ALU.add), r=[bta, btb], w=[bx])

    def load_stream(src_ap, bsrc, view):
        slot, bslot = stream.next()
        sv = view(slot)
        O("sp", lambda e: e.dma_start(out=sv, in_=src_ap), r=[bsrc], w=[bslot], dma=True)
        return sv, bslot

    v8x512 = lambda slot: slot[:, :].rearrange("p (k n) -> p k n", k=8)
    vgu = lambda slot: slot[:, :].rearrange("p (a k n) -> p a k n", a=2, k=8)

    def inproj_mm(wv, bw, s):
        bank, bb = nbank()
        for kc in range(8):
            O("pe", lambda e, kc=kc: e.matmul(bank[:, :], lhsT=nT[:, kc, s * 128:(s + 1) * 128], rhs=wv[:, kc, :], start=(kc == 0), stop=(kc == 7)),
              r=[bnT[s], bw], w=[bb])
        return bank, bb

    def qk_post(bank, bb, isq, rp, brp):
        sq, bsq = T.next()
        O("act", lambda e: e.activation(out=sq[:], in_=bank[:, :], func=AF.Square), r=[bb], w=[bsq])
        ssq, bssq = tiny.next()
        O("dve", lambda e: e.tensor_reduce(out=ssq[:, 0:8], in_=sq[:, :].rearrange("p (g d) -> p g d", g=8), axis=AX.X, op=ALU.add), r=[bsq], w=[bssq])
        rstd_inplace(ssq, bssq, 8, 1.0, 64 * EPS)
        qn, bqn = T.next()
        qn3 = qn[:, :].rearrange("p (g d) -> p g d", g=8)
        O("dve", lambda e: e.tensor_tensor(out=qn3, in0=bank[:, :].rearrange("p (g d) -> p g d", g=8), in1=bc(ssq[:, 0:8], [128, 8, 64], 2), op=ALU.mult),
          r=[bb, bssq], w=[bqn])
        if isq:
            O("dve", lambda e: e.tensor_tensor(out=qn3, in0=qn3, in1=bc(gq, [128, 8, 64], 1), op=ALU.mult), r=[bqn, bconst], w=[bqn])
        else:
            O("dve", lambda e: e.scalar_tensor_tensor(out=qn3, in0=qn3, scalar=8.0, in1=bc(gk, [128, 8, 64], 1), op0=ALU.mult, op1=ALU.mult), r=[bqn, bconst], w=[bqn])
        rope(qn3, bqn, 8, 8, rp[:, 0:8], rp[:, 8:16], brp)
        return qn, bqn

    def transpose4(src, bsrc):
        bank, bb = nbank()
        tp = bank[:].bitcast(BF16)
        for j in range(4):
            O("pe", lambda e, j=j: e.transpose(out=tp[:, j * 128:(j + 1) * 128], in_=src[:, j * 128:(j + 1) * 128], identity=ident), r=[bsrc, bconst], w=[bb])
        return tp[:, 0:512].rearrange("p (j t) -> p j t", j=4), bb

    def state_update(st_list, kw_list, rv, brv, acol):
        for st, (kw, bkw) in zip(st_list, kw_list):
            S, bSx = st["S"], st["bS"]
            bankU, bbU = nbank()
            for h in range(4):
                O("pe", lambda e, h=h, kw=kw, bankU=bankU: e.matmul(bankU[:, h * 128:(h + 1) * 128], lhsT=kw[:, (h // 2) * 128:(h // 2 + 1) * 128], rhs=rv[:, h * 128:(h + 1) * 128], start=True, stop=True), r=[bkw, brv], w=[bbU])
            for h in range(4):
                r0 = 64 * (h % 2)
                O("dve", lambda e, h=h, r0=r0, S=S, bankU=bankU: e.scalar_tensor_tensor(out=S[r0:r0 + 64, h // 2, :], in0=S[r0:r0 + 64, h // 2, :], scalar=acol[r0:r0 + 64, h // 2:h // 2 + 1],
                                                                                    in1=bankU[r0:r0 + 64, h * 128:(h + 1) * 128], op0=ALU.mult, op1=ALU.add), r=[bbU, bSx, bconst], w=[bSx])
            st["cur"] = (st["cur"] + 1) % len(st["Sbf"])
            Sbf, bSbx = st["Sbf"][st["cur"]], st["bSb"][st["cur"]]
            O("dve", lambda e, S=S, Sbf=Sbf: e.tensor_copy(out=Sbf[:], in_=S[:]), r=[bSx], w=[bSbx])

    def retention_out_gen(s, st_prev, tabs, qt, bqt, rk, brk, rv, brv):
        dm, cross, acol = tabs
        bankA, bbA = nbank()
        for h in range(4):
            O("pe", lambda e, h=h: e.matmul(bankA[:, h * 128:(h + 1) * 128], lhsT=rk[:, h // 2, :], rhs=qt[:, h, :], start=True, stop=True), r=[brk, bqt], w=[bbA])
        at, bat = Tb.next()
        O("dve", lambda e: e.tensor_tensor(out=at[:], in0=bankA[:, :], in1=dm, op=ALU.mult), r=[bbA, bdmask], w=[bat])
        yield
        bankO, bbO = nbank()
        for h in range(4):
            O("pe", lambda e, h=h: e.matmul(bankO[:, h * 128:(h + 1) * 128], lhsT=rv[:, h * 128:(h + 1) * 128], rhs=at[:, h * 128:(h + 1) * 128], start=True, stop=False), r=[brv, bat], w=[bbO])
            for (Sbf, bSbx, c0, c1) in st_prev:
                last = (c1 == 128)
                O("pe", lambda e, h=h, Sbf=Sbf, c0=c0, c1=c1, last=last: e.matmul(bankO[:, h * 128 + c0:h * 128 + c1], lhsT=Sbf[:, h // 2, :], rhs=qt[:, h, c0:c1], start=False, stop=last),
                  r=[bSbx, bqt], w=[bbO])
        osq, bosq = Tb.next()
        O("act", lambda e: e.activation(out=osq[:], in_=bankO[:, :], func=AF.Square), r=[bbO], w=[bosq])
        yield
        bankS, bbS = nbank()
        O("pe", lambda e: e.matmul(bankS[:, :], lhsT=ones128, rhs=osq[:], start=True, stop=True), r=[bosq, bconst], w=[bbS])
        rs, brs = T.next()
        O("act", lambda e: e.activation(out=rs[:], in_=bankS[:, :], func=AF.Ln, scale=1.0, bias=128 * EPS), r=[bbS], w=[brs])
        O("act", lambda e: e.activation(out=rs[:], in_=rs[:], func=AF.Exp, scale=-0.5), r=[brs], w=[brs])
        O("dve", lambda e: e.scalar_tensor_tensor(out=rs[:], in0=bankO[:, :], scalar=math.sqrt(128.0), in1=rs[:], op0=ALU.mult, op1=ALU.mult), r=[bbO, brs], w=[brs])
        O("pool", lambda e: e.tensor_tensor(out=mixT[:, 4:8, s * 128:(s + 1) * 128], in0=rs[:, :].rearrange("p (h t) -> p h t", h=4), in1=sgT[:, :, s * 128:(s + 1) * 128], op=ALU.mult),
          r=[brs] + bsg, w=bmix[4:8])

    def attention(h, N, q0, blocks, out_c0, pend=None):
        accs = [(banks[4], bbank[4]), (banks[5], bbank[5])]
        zb, bzb = banks[6], bbank[6]
        scr, bscr = banks[7], bbank[7]
        ssum, bssum = T.t[0], T.b[0]
        ones_of = {128: ones128, 16: ones16, 80: ones80}
        nb = len(blocks)
        sb = {}
        pend = list(pend) if pend else []

        def qk(k):
            kt, bkt, va, bva, nv, lo, diag = blocks[k]
            pair = []
            for c in range(2):
                bank, bb = banks[2 * (k % 2) + c], bbank[2 * (k % 2) + c]
                O("pe", lambda e, c=c, bank=bank, lo=lo, kt=kt, diag=diag: e.matmul(bank[:, lo:N], lhsT=kt[64 * c:64 * c + 64, :], rhs=qpad[64 * c:64 * c + 64, h, c, q0 + lo:q0 + N], start=True, stop=not diag), r=[bkt] + bqpad, w=[bb])
                if diag:
                    O("pe", lambda e, c=c, bank=bank, lo=lo: e.matmul(bank[:, lo:lo + 64], lhsT=maskL[64 * c:64 * c + 64, :], rhs=maskR[64 * c:64 * c + 64, :], start=False, stop=True), r=[bconst], w=[bb])
                pair.append((bank, bb))
            sb[k] = pair

        def ex(k):
            kt, bkt, va, bva, nv, lo, diag = blocks[k]
            pt, bpt = Pb.next()
            pr = pairs[k % 2]
            O("act", lambda e, pr=pr, lo=lo, pt=pt: e.activation(out=pt[:, :, lo:N], in_=pr[:, :, lo:N], func=AF.Exp), r=[sb[k][0][1], sb[k][1][1]], w=[bpt])
            sb[k] = (pt, bpt)
            if k == 0:
                O("dve", lambda e: e.memset(ssum[:, 0:N], 0.0), w=[bssum])
            O("dve", lambda e, pt=pt, nv=nv, lo=lo: e.tensor_tensor(out=ssum[0:nv, lo:N], in0=ssum[0:nv, lo:N], in1=pt[0:nv, 1, lo:N], op=ALU.add), r=[bpt, bssum], w=[bssum])

        def pv(k):
            kt, bkt, va, bva, nv, lo, diag = blocks[k]
            pt, bpt = sb.pop(k)
            first = (k == 0); last = (k == nb - 1)
            for c in range(2):
                O("pe", lambda e, c=c, lo=lo, va=va, pt=pt, first=first, last=last: e.matmul(accs[c][0][:, lo:N], lhsT=va, rhs=pt[:, c, lo:N], start=first, stop=last), r=[bva, bpt], w=[accs[c][1]])
            on = ones_of[nv]
            O("pe", lambda e, lo=lo, on=on, pt=pt, first=first, last=last: e.matmul(zb[:, lo:N], lhsT=on, rhs=pt[:, 0, lo:N], start=first, stop=last), r=[bconst, bpt], w=[bzb])

        qk(0)
        for k in range(nb):
            if k + 1 < nb:
                qk(k + 1)
            ex(k)
            pv(k)
            if k >= 1 and pend:
                pend.pop(0)()
        while pend:
            pend.pop(0)()

        a0c, ba0c = T.t[1], T.b[1]
        a1c, ba1c = T.t[2], T.b[2]
        z0c, bz0c = T.t[3], T.b[3]
        rz0, brz0 = T.t[4], T.b[4]
        rz1, brz1 = T.t[5], T.b[5]
        O("act", lambda e: e.activation(out=a0c[:, 0:N], in_=accs[0][0][:, 0:N], func=AF.Copy), r=[accs[0][1]], w=[ba0c])
        O("act", lambda e: e.activation(out=a1c[:, 0:N], in_=accs[1][0][:, 0:N], func=AF.Copy), r=[accs[1][1]], w=[ba1c])
        O("act", lambda e: e.activation(out=z0c[:, 0:N], in_=zb[:, 0:N], func=AF.Ln), r=[bzb], w=[bz0c])
        sbf, bsbf = Tb.next()
        O("dve", lambda e: e.tensor_copy(out=sbf[:, 0:N], in_=ssum[:, 0:N]), r=[bssum], w=[bsbf])
        hold = {}

        def t1():
            O("pe", lambda e: e.matmul(scr[:, 0:N], lhsT=ones128, rhs=sbf[:, 0:N], start=True, stop=True), r=[bsbf, bconst], w=[bscr])
            O("act", lambda e: e.activation(out=rz0[:, 0:N], in_=z0c[:, 0:N], func=AF.Exp, scale=-1.0), r=[bz0c], w=[brz0])

        def t2():
            O("act", lambda e: e.activation(out=rz1[:, 0:N], in_=scr[:, 0:N], func=AF.Ln), r=[bscr], w=[brz1])
            O("act", lambda e: e.activation(out=rz1[:, 0:N], in_=rz1[:, 0:N], func=AF.Exp, scale=-1.0), r=[brz1], w=[brz1])
            O("dve", lambda e: e.tensor_tensor(out=rz0[:, 0:N], in0=a0c[:, 0:N], in1=rz0[:, 0:N], op=ALU.mult), r=[ba0c, brz0], w=[brz0])

        def t3():
            O("dve", lambda e: e.tensor_tensor(out=rz1[:, 0:N], in0=a1c[:, 0:N], in1=rz1[:, 0:N], op=ALU.mult), r=[ba1c, brz1], w=[brz1])
            O("dve", lambda e: e.scalar_tensor_tensor(out=rz0[:, 0:N], in0=rz1[:, 0:N], scalar=neglam[:, 0:1], in1=rz0[:, 0:N], op0=ALU.mult, op1=ALU.add), r=[brz0, brz1, bconst], w=[brz0])

        def t4():
            osq, bosq = Tb.next()
            O("act", lambda e: e.activation(out=osq[:, 0:N], in_=rz0[:, 0:N], func=AF.Square), r=[brz0], w=[bosq])
            hold["osq"] = (osq, bosq)

        def t5():
            osq, bosq = hold["osq"]
            O("pe", lambda e: e.matmul(scr[:, 0:N], lhsT=ones128, rhs=osq[:, 0:N], start=True, stop=True), r=[bosq, bconst], w=[bscr])

        def t6():
            O("act", lambda e: e.activation(out=rz1[:, 0:N], in_=scr[:, 0:N], func=AF.Ln, scale=1.0, bias=128 * EPS), r=[bscr], w=[brz1])
            O("act", lambda e: e.activation(out=rz1[:, 0:N], in_=rz1[:, 0:N], func=AF.Exp, scale=-0.5), r=[brz1], w=[brz1])

        def t7():
            O("dve", lambda e: e.scalar_tensor_tensor(out=mixT[:, h, out_c0:out_c0 + N], in0=rz0[:, 0:N], scalar=gsubc[:, 0:1], in1=rz1[:, 0:N], op0=ALU.mult, op1=ALU.mult),
              r=[brz0, brz1, bconst], w=[bmix[h]])
        return [t1, t2, t3, t4, t5, t6, t7]

    def out_ffn(nsub, x_src, y_dst, ykey, wouts, wgu_pre, fpre, pend=()):
        N = nsub * 128

        hts = {}
        xns = {}

        def F_mm(s):
            if s in fpre:
                ht, bht = fpre[s]
            else:
                ht, bht = xpool.next()
                O("sp", lambda e: e.dma_start(out=ht[:], in_=x_src(s)), w=[bht], dma=True)
            for half in range(2):
                wv, bw = wouts[half]
                bank, bb = nbank()
                for c in range(8):
                    O("pe", lambda e, c=c, bank=bank, wv=wv: e.matmul(bank[:, :], lhsT=mixT[:, c, s * 128:(s + 1) * 128], rhs=wv[:, c, :], start=(c == 0), stop=(c == 7)), r=[bmix[c], bw], w=[bb])
                O("dve", lambda e, half=half, bank=bank: e.tensor_tensor(out=ht[:, half * 512:(half + 1) * 512], in0=bank[:, :], in1=ht[:, half * 512:(half + 1) * 512], op=ALU.add), r=[bb, bht], w=[bht])
            byp[(ykey, s)] = Buf()
            O("sp", lambda e: e.dma_start(out=y_dst(s), in_=ht[:]), r=[bht], w=[byp[(ykey, s)]], dma=True)
            hts[s] = (ht, bht)

        def F_p1(s):
            xns[s] = norm_part1(*hts[s])

        def F_p2(s):
            norm_part2(xns[s][0], xns[s][1], gcols[:, 8:16], s)

        pend = list(pend)
        if nsub == 4:
            def F_grp(j, cs, first, pops=()):
                s_, half = divmod(j, 2)
                wv, bw = wouts[half]
                for n_, c in enumerate(cs):
                    O("pe", lambda e, c=c, j=j, s_=s_, wv=wv, first=first: e.matmul(banks[j][:, :], lhsT=mixT[:, c, s_ * 128:(s_ + 1) * 128], rhs=wv[:, c, :],
                                                                                      start=(first and c == cs[0]), stop=(c == 3)), r=[bmix[c], bw], w=[bbank[j]])
                    if n_ in pops and pend:
                        pend.pop(0)()

            def F_add(s):
                if s in fpre:
                    ht, bht = fpre[s]
                else:
                    ht, bht = xpool.next()
                    O("sp", lambda e: e.dma_start(out=ht[:], in_=x_src(s)), w=[bht], dma=True)
                for half in range(2):
                    bank, bb = banks[2 * s + half], bbank[2 * s + half]
                    O("dve", lambda e, half=half, bank=bank: e.tensor_tensor(out=ht[:, half * 512:(half + 1) * 512], in0=bank[:, :], in1=ht[:, half * 512:(half + 1) * 512], op=ALU.add), r=[bb, bht], w=[bht])
                byp[(ykey, s)] = Buf()
                O("sp", lambda e: e.dma_start(out=y_dst(s), in_=ht[:]), r=[bht], w=[byp[(ykey, s)]], dma=True)
                hts[s] = (ht, bht)

            for j in range(4):
                F_grp(j, (4, 5, 6, 7, 0, 1, 2), True, pops=(3, 6))
            while pend:
                pend.pop(0)()
            for j in range(4):
                F_grp(j, (3,), False)
            F_add(0); F_p1(0); F_add(1); F_p1(1)
            for j in range(4, 8):
                F_grp(j, (4, 5, 6, 7, 0, 1, 2, 3), True)
            for step in (("2", 0), ("a", 2), ("1", 2), ("2", 1), ("a", 3), ("1", 3), ("2", 2), ("2", 3)):
                {"a": F_add, "1": F_p1, "2": F_p2}[step[0]](step[1])
        else:
            while pend:
                pend.pop(0)()
            F_mm(0); F_p1(0); F_p2(0)
        for c in range(11):
            if c in wgu_pre:
                wv, bw = wgu_pre[c]
            else:
                wv, bw = load_stream(Wgu_s[c], bWgu[c], vgu)
            for j in range(2):
                fc = 2 * c + j
                bg, bbg = nbank()
                bu, bbu = nbank()
                for a, (bank, bb) in enumerate(((bg, bbg), (bu, bbu))):
                    for kc in range(8):
                        O("pe", lambda e, a=a, kc=kc, j=j, bank=bank, wv=wv: e.matmul(bank[:, 0:N], lhsT=wv[:, a, kc, j * 128:(j + 1) * 128], rhs=nT[:, kc, 0:N], start=(kc == 0), stop=(kc == 7)),
                          r=[bw] + bnT[0:nsub], w=[bb])
                sg, bsgt = T.next()
                O("act", lambda e, sg=sg, bg=bg: e.activation(out=sg[:, 0:N], in_=bg[:, 0:N], func=AF.Silu), r=[bbg], w=[bsgt])
                O("dve", lambda e, sg=sg, bu=bu, fc=fc: e.tensor_tensor(out=actT[:, fc, 0:N], in0=bu[:, 0:N], in1=sg[:, 0:N], op=ALU.mult), r=[bbu, bsgt], w=[bact[fc]])
        prefetch_next()
        accs = [(banks[j], bbank[j]) for j in range(2 * nsub)]
        for c in range(6):
            nf = 4 if c < 5 else 2
            wv, bw = load_stream(Wd_s[c * 512:c * 512 + nf * 128, :].rearrange("(f p) n -> p f n", p=128), bWd[c],
                                 lambda slot, nf=nf: slot[:, 0:nf * 1024].rearrange("p (f n) -> p f n", f=nf))
            for f in range(nf):
                fc = 4 * c + f
                for s in range(nsub):
                    for half in range(2):
                        bank, bb = accs[2 * s + half]
                        O("pe", lambda e, fc=fc, f=f, s=s, half=half, bank=bank, wv=wv: e.matmul(bank[:, :], lhsT=actT[:, fc, s * 128:(s + 1) * 128], rhs=wv[:, f, half * 512:(half + 1) * 512],
                                                                                            start=(fc == 0), stop=(fc == NFC - 1)), r=[bact[fc], bw], w=[bb])
        order = (2, 3, 0, 1) if nsub == 4 else (0,)
        hv = {}
        for s in order:
            ht = actT[:, 4 * s:4 * s + 4, :].rearrange("p a n -> p (a n)").bitcast(F32)
            bht = bact[4 * s:4 * s + 4]
            O("sp", lambda e, ht=ht, s=s: e.dma_start(out=ht, in_=y_dst(s)), r=[byp[(ykey, s)]], w=bht, dma=True)
            hv[s] = (ht, bht)
        prefetch_weights_next()
        for s in order:
            ht, bht = hv[s]
            for half in range(2):
                bank, bb = accs[2 * s + half]
                O("dve", lambda e, ht=ht, half=half, bank=bank: e.tensor_tensor(out=ht[:, half * 512:(half + 1) * 512], in0=bank[:, :], in1=ht[:, half * 512:(half + 1) * 512], op=ALU.add), r=[bb] + bht, w=bht)
            O("sp", lambda e, ht=ht, s=s: e.dma_start(out=y_dst(s), in_=ht), r=bht, w=[byp[(ykey, s)]], dma=True)

    def tile(kind, b=0, i=0):
        nsub = 4 if kind == "prompt" else 1
        N = nsub * 128
        rt, brt = ropeT.next()
        t0 = {"meta": 0, "prompt": 1 + 4 * i, "sample": 33}[kind]
        O("sp", lambda e: e.dma_start(out=rt[:, 0:nsub, :], in_=rope_d[:, t0 * 80:(t0 + nsub) * 80].rearrange("p (s c) -> p s c", s=nsub)), w=[brt], dma=True)
        W = {}
        pw = PREW.pop((kind, b, i), None)
        if pw is not None:
            wg, bwg = pw["wg"]; wq, bwq = pw["wq"]; wr, bwr = pw["wr"]; W[0] = pw["w0"]
        else:
            if kind != "meta":
                wg, bwg = load_stream(Win_s[5], bWin[5], v8x512)
            wq, bwq = load_stream(Win_s[3], bWin[3], v8x512)
            wr, bwr = load_stream(Win_s[4], bWin[4], v8x512)
            if kind != "meta":
                W[0] = load_stream(Win_s[0], bWin[0], v8x512)

        def gen_A(s):
            pre = PRE.pop((kind, b, i, s), None)
            if pre is not None and pre[0] == "n":
                norm_part2(pre[1], pre[2], gcols[:, 0:8], s)
                return
                yield
            if pre is not None:
                xt, bx = pre[1], pre[2]
            else:
                xt, bx = xpool.next()
                src = x_source(kind, b, i, s)
                O("sp", lambda e: e.dma_start(out=xt[:], in_=src), w=[bx], dma=True)
            yield from norm_transpose_gen(xt, bx, gcols[:, 0:8], s)

        def gen_G():
            gq_ = []
            for h in range(4):
                bank, bb = nbank()
                for kc in range(8):
                    O("pe", lambda e, h=h, kc=kc, bank=bank: e.matmul(bank[:, 0:N], lhsT=wg[:, kc, h * 128:(h + 1) * 128], rhs=nT[:, kc, 0:N], start=(kc == 0), stop=(kc == 7)), r=[bwg] + bnT[0:nsub], w=[bb])
                gq_.append((h, bank, bb))
            for (h, bank, bb) in gq_:
                O("act", lambda e, h=h, bank=bank: e.activation(out=sgT[:, h, 0:N], in_=bank[:, 0:N], func=AF.Silu), r=[bb], w=[bsg[h]])
            return
            yield
        if kind == "prompt":
            tabs = (dmask[:, :], crossP, acolP)
            sts_ = [(ST[0], 0, 128)]
            w8l = [w8P]
        elif kind == "sample":
            tabs = (dmask[:, :], crossS, acolS)
            sts_ = [(ST[0], 0, 64), (ST[1], 64, 128)]
            w8l = [w8S0, w8S1]
        else:
            tabs = (dmask[:, :], crossP, acol0)
            sts_ = [(ST[2], 0, 128)]
            w8l = [w8M]

        RS = {}

        def gen_R1(s):
            rp = rt[:, s, :]
            bankq, bbq = inproj_mm(wq, bwq, s)
            bankr, bbr = inproj_mm(wr, bwr, s)
            rqk, brqk = T.next()
            O("act", lambda e: e.activation(out=rqk[:], in_=bankq[:, :], func=AF.Copy), r=[bbq], w=[brqk])
            rqk3 = rqk[:, :].rearrange("p (g d) -> p g d", g=8)
            rv, brv = rvb.next()
            O("act", lambda e: e.activation(out=rv[:], in_=bankr[:, :], func=AF.Copy), r=[bbr], w=[brv])
            rope(rqk3, brqk, 8, 32, rp[:, 16:48], rp[:, 48:80], brt)
            qtb, bqtb = Tb.next()
            O("dve", lambda e: e.tensor_tensor(out=qtb[:, 0:256].rearrange("p (h d) -> p h d", h=4), in0=rqk3[:, 0:4, :], in1=bc(tabs[1], [128, 4, 64], 2), op=ALU.mult), r=[brqk, bconst], w=[bqtb])
            O("dve", lambda e: e.tensor_scalar(out=qtb[:, 256:512], in0=rqk[:, 256:512], scalar1=0.125, scalar2=None, op0=ALU.mult), r=[brqk], w=[bqtb])
            kws = []
            for w8 in w8l:
                kw, bkw = kwb.next()
                O("dve", lambda e, kw=kw, w8=w8: e.tensor_tensor(out=kw[:, :].rearrange("p (h d) -> p h d", h=4), in0=rqk3[:, 4:8, :], in1=bc(w8, [128, 4, 64], 2), op=ALU.mult), r=[brqk, bconst], w=[bkw])
                kws.append((kw, bkw))
            yield
            tp3, bbt = transpose4(qtb, bqtb)
            qt, bqt = qtpad.next()
            qt3 = qt[:, :, :].rearrange("p (a two) t -> p a two t", two=2)
            O("dve", lambda e: e.tensor_copy(out=qt3[0:64, :, 0, :], in_=tp3[0:64, 0:2, :]), r=[bbt], w=[bqt])
            O("act", lambda e: e.activation(out=qt3[64:128, :, 1, :], in_=tp3[64:128, 0:2, :], func=AF.Copy), r=[bbt], w=[bqt])
            rk, brk = rkT.next()
            O("dve", lambda e: e.tensor_copy(out=rk[:], in_=tp3[:, 2:4, :]), r=[bbt], w=[brk])
            st_prev = [(st["Sbf"][st["cur"]], st["bSb"][st["cur"]], c0, c1) for (st, c0, c1) in sts_]
            state_update([st for (st, c0, c1) in sts_], kws, rv, brv, tabs[2])
            RS[s] = (st_prev, qt, bqt, rk, brk, rv, brv)

        def gen_R2(s):
            st_prev, qt, bqt, rk, brk, rv, brv = RS[s]
            yield from retention_out_gen(s, st_prev, tabs, qt, bqt, rk, brk, rv, brv)

        def gen_QKV(c, s, wv, bw):
            rp = rt[:, s, :]
            bank, bb = inproj_mm(wv, bw, s)
            tok0 = i * 512 + s * 128
            blk = 1 + 4 * i + s
            if c == 0:
                qn, bqn = qk_post(bank, bb, True, rp, brt)
                qb, bqb = Tb.next()
                O("act", lambda e: e.activation(out=qb[:], in_=qn[:], func=AF.Copy), r=[bqn], w=[bqb])
                yield
                tp3, bbt = transpose4(qb, bqb)
                O("dve", lambda e: e.tensor_copy(out=qpad[0:64, :, 0, s * 128:(s + 1) * 128], in_=tp3[0:64, :, :]), r=[bbt], w=[bqpad[s]])
                O("act", lambda e: e.activation(out=qpad[64:128, :, 1, s * 128:(s + 1) * 128], in_=tp3[64:128, :, :], func=AF.Copy), r=[bbt], w=[bqpad[s]])
            elif c == 1:
                kn, bkn = qk_post(bank, bb, False, rp, brt)
                if kind == "prompt":
                    O("sp", lambda e: e.dma_start(out=kp[b, NM + tok0:NM + tok0 + 128, :], in_=kn[:]), r=[bkn], dma=True)
                elif kind == "meta":
                    for bq in range(2):
                        O("sp", lambda e, bq=bq: e.dma_start(out=kp[bq, 0:NM, :], in_=kn[0:NM, :]), r=[bkn], dma=True)
                else:
                    O("sp", lambda e: e.dma_start(out=ks, in_=kn[:]), r=[bkn], w=[bks], dma=True)
                if kind != "sample":
                    kb, bkb = Tb.next()
                    O("act", lambda e: e.activation(out=kb[:], in_=kn[:], func=AF.Copy), r=[bkn], w=[bkb])
                    yield
                    tp3, bbt = transpose4(kb, bkb)
                    kblk = 0 if kind == "meta" else blk
                    O("dve", lambda e: e.tensor_copy(out=kT[:, :, kblk * 128:(kblk + 1) * 128], in_=tp3), r=[bbt], w=[bkT[kblk]])
            else:
                vf, bvf = T.next()
                O("act", lambda e: e.activation(out=vf[:], in_=bank[:, :], func=AF.Copy), r=[bb], w=[bvf])
                if kind == "prompt":
                    O("sp", lambda e: e.dma_start(out=vp[b, NM + tok0:NM + tok0 + 128, :], in_=vf[:]), r=[bvf], dma=True)
                elif kind == "meta":
                    for bq in range(2):
                        O("sp", lambda e, bq=bq: e.dma_start(out=vp[bq, 0:NM, :], in_=vf[0:NM, :]), r=[bvf], dma=True)
                else:
                    O("sp", lambda e: e.dma_start(out=vs, in_=vf[:]), r=[bvf], w=[bvs], dma=True)
                if kind != "sample":
                    vblk = 0 if kind == "meta" else blk
                    O("pool", lambda e: e.tensor_copy(out=v_sb[:, vblk, :], in_=vf[:]), r=[bvf], w=[bv[vblk]])

        def gen_load(c):
            W[c] = load_stream(Win_s[c], bWin[c], v8x512)
            return
            yield

        items = []
        idx = {}

        def add(name, gen, deps, typ, ring="X"):
            idx[name] = len(items)
            items.append(dict(gen=gen, deps=[idx[d] for d in deps], typ=typ, ring=ring))

        for s in range(nsub):
            add(("A", s), (lambda s=s: gen_A(s)), [], "A")
        if kind != "meta":
            add("G", gen_G, [("A", s) for s in range(nsub)], "G")
        for s in range(min(2, nsub)):
            add(("R1", s), (lambda s=s: gen_R1(s)), [("A", s)], "R", "R")
        for s in range(nsub):
            if kind != "meta":
                add(("R2", s), (lambda s=s: gen_R2(s)), [("R1", s), "G"], "R", "R")
            if s + 2 < nsub:
                add(("R1", s + 2), (lambda s=s: gen_R1(s + 2)), [("A", s + 2)] + ([("R2", s)] if kind != "meta" else []), "R", "R")
        if kind != "meta":
            for s in range(nsub):
                add(("Q", s), (lambda s=s: gen_QKV(0, s, *W[0])), [("A", s)], "Q")
        add("LK", (lambda: gen_load(1)), (["G"] if kind != "meta" else [("R1", s) for s in range(nsub)]), "L")
        for s in range(nsub):
            add(("K", s), (lambda s=s: gen_QKV(1, s, *W[1])), [("A", s), "LK"], "K")
        add("LV", (lambda: gen_load(2)), [("R1", s) for s in range(nsub)] + ["LK"], "L")
        for s in range(nsub):
            add(("V", s), (lambda s=s: gen_QKV(2, s, *W[2])), [("A", s), "LV"], "V")
        run_dag(items, width=6, cap=3)
        if kind == "meta":
            return
        wouts = [load_stream(Wout_s[half], bWout[half], v8x512) for half in range(2)]
        wgu_pre = {c: load_stream(Wgu_s[c], bWgu[c], vgu) for c in range(2)}
        fpre = {}
        for s_ in range(min(3, nsub)):
            xt_, bx_ = xpool.next()
            O("sp", lambda e, xt_=xt_, s_=s_: e.dma_start(out=xt_[:], in_=x_source(kind, b, i, s_)), w=[bx_], dma=True)
            fpre[s_] = (xt_, bx_)
        if kind == "prompt":
            tail = []
            for h in range(4):
                blocks = [(kT[:, h, 0:128], bkT[0], v_sb[:, 0, h * 128:(h + 1) * 128], bv[0], 16, 0, False)]
                for j in range(4 * i):
                    blocks.append((kT[:, h, (1 + j) * 128:(2 + j) * 128], bkT[1 + j], v_sb[:, 1 + j, h * 128:(h + 1) * 128], bv[1 + j], 128, 0, False))
                for jj in range(4):
                    j = 4 * i + jj
                    blocks.append((kT[:, h, (1 + j) * 128:(2 + j) * 128], bkT[1 + j], v_sb[:, 1 + j, h * 128:(h + 1) * 128], bv[1 + j], 128, 128 * jj, True))
                tail = attention(h, 512, 0, blocks, 0, tail)
            out_ffn(4, lambda s: xp[b, i * 512 + s * 128:i * 512 + (s + 1) * 128, :], lambda s: yp[b, i * 512 + s * 128:i * 512 + (s + 1) * 128, :], ("p", b, i), wouts, wgu_pre, fpre, tail)
        else:
            def sample_stream(sbi, tail):
                for f_ in tail:
                    f_()
                tail = []
                for blk in range(33):
                    kf, bkf = T.next()
                    vf, bvf = T.next()
                    if blk < 32:
                        O("sp", lambda e, kf=kf, blk=blk: e.dma_start(out=kf[:], in_=ck[sbi, blk * 128:(blk + 1) * 128, :]), w=[bkf], dma=True)
                        O("sp", lambda e, vf=vf, blk=blk: e.dma_start(out=vf[:], in_=cv[sbi, blk * 128:(blk + 1) * 128, :]), w=[bvf], dma=True)
                    else:
                        O("pool", lambda e, kf=kf: e.memset(kf[:], 0.0), w=[bkf])
                        O("pool", lambda e, vf=vf: e.memset(vf[:], 0.0), w=[bvf])
                        O("sp", lambda e, kf=kf: e.dma_start(out=kf[0:NM, :], in_=ck[sbi, PAST:LCACHE, :]), w=[bkf], dma=True)
                        O("sp", lambda e, vf=vf: e.dma_start(out=vf[0:NM, :], in_=cv[sbi, PAST:LCACHE, :]), w=[bvf], dma=True)
                        O("sp", lambda e, kf=kf: e.dma_start(out=kf[NM:NM + DEC, :], in_=ks[sbi * DEC:(sbi + 1) * DEC, :]), r=[bks], w=[bkf], dma=True)
                        O("sp", lambda e, vf=vf: e.dma_start(out=vf[NM:NM + DEC, :], in_=vs[sbi * DEC:(sbi + 1) * DEC, :]), r=[bvs], w=[bvf], dma=True)
                    kb, bkb = Tb.next()
                    O("pool", lambda e, kb=kb, kf=kf: e.tensor_copy(out=kb[:], in_=kf[:]), r=[bkf], w=[bkb])
                    tp3, bbt = transpose4(kb, bkb)
                    O("dve", lambda e, tp3=tp3, blk=blk: e.tensor_copy(out=kT[:, :, blk * 128:(blk + 1) * 128], in_=tp3), r=[bbt], w=[bkT[blk]])
                    O("act", lambda e, vf=vf, blk=blk: e.activation(out=v_sb[:, blk, :], in_=vf[:], func=AF.Copy), r=[bvf], w=[bv[blk]])
                for h in range(4):
                    blocks = []
                    for j in range(33):
                        blocks.append((kT[:, h, j * 128:(j + 1) * 128], bkT[j], v_sb[:, j, h * 128:(h + 1) * 128], bv[j], 128 if j < 32 else 80, 0, False))
                    tail = attention(h, DEC, sbi * DEC, blocks, sbi * DEC, tail)
                return tail
            tail = []
            for sbi_ in range(2):
                tail = sample_stream(sbi_, tail)
            for f_ in tail:
                f_()
            out_ffn(1, lambda s: xs, lambda s: ys, ("s", 0, 0), wouts, wgu_pre, fpre)

    if DEBUG_STOP == "pro":
        P.emit(nc)
        return nc
    SCHED["list"] = [("meta", 0, 0)] + [("prompt", b_, i_) for b_ in range(2) for i_ in range(SEQ // 512)] + [("sample", 0, 0)]
    try:
        tile("meta")
    except _Stop:
        P.emit(nc)
        return nc
    if DEBUG_STOP == "meta":
        P.emit(nc)
        return nc
    for b in range(2):
        O("dve", lambda e: e.tensor_copy(out=ST[0]["S"][:], in_=ST[2]["S"][:]), r=[ST[2]["bS"]], w=[ST[0]["bS"]])
        O("pool", lambda e, c_=ST[0]["cur"]: e.tensor_copy(out=ST[0]["Sbf"][c_][:], in_=ST[2]["S"][:]), r=[ST[2]["bS"]], w=[ST[0]["bSb"][ST[0]["cur"]]])
        for i in range(SEQ // 512):
            if isinstance(DEBUG_STOP, int) and b * 8 + i >= DEBUG_STOP:
                P.emit(nc)
                return nc
            SCHED["pos"] = 1 + b * (SEQ // 512) + i
            tile("prompt", b, i)
        for pr in range(2):
            O("sp", lambda e, pr=pr, b=b: e.dma_start(out=stp[b, 2 * pr:2 * pr + 2].rearrange("h k v -> (h k) v"), in_=ST[0]["S"][:, pr, :]), r=[ST[0]["bS"]], dma=True)
    O("sp", lambda e: e.dma_start(out=dmask[:], in_=dmask_d[:, 512:1024]), w=[bdmask], dma=True)
    for sbi in range(2):
        for pr in range(2):
            O("sp", lambda e, pr=pr, sbi=sbi: e.dma_start(out=ST[sbi]["S"][:, pr, :], in_=st_in[sbi, 2 * pr:2 * pr + 2].rearrange("h k v -> (h k) v")), w=[ST[sbi]["bS"]], dma=True)
        O("pool", lambda e, sbi=sbi, c_=ST[sbi]["cur"]: e.tensor_copy(out=ST[sbi]["Sbf"][c_][:], in_=ST[sbi]["S"][:]), r=[ST[sbi]["bS"]], w=[ST[sbi]["bSb"][ST[sbi]["cur"]]])
    SCHED["pos"] = len(SCHED["list"]) - 1
    tile("sample")
    for sbi in range(2):
        for pr in range(2):
            O("sp", lambda e, pr=pr, sbi=sbi: e.dma_start(out=sts[sbi, 2 * pr:2 * pr + 2].rearrange("h k v -> (h k) v"), in_=ST[sbi]["S"][:, pr, :]), r=[ST[sbi]["bS"]], dma=True)
    P.emit(nc)
    return nc


_CACHE = {}


def kernel(x_prompt, x_sample, cache_k, cache_v, state_ret, meta, g_mix, w_in, g_q, g_k,
           lam_q1, lam_k1, lam_q2, lam_k2, g_sub, w_out, g_ffn, w_gate, w_up, w_down):
    f32 = np.float32
    A = lambda a: np.ascontiguousarray(np.asarray(a, dtype=f32))
    if "nc" not in _CACHE:
        _CACHE["nc"] = build_nc()
        _CACHE["consts"] = _host_consts()
    nc = _CACHE["nc"]
    cst = _CACHE["consts"]
    x_prompt = A(x_prompt); x_sample = A(x_sample); cache_k = A(cache_k); cache_v = A(cache_v); state_ret = A(state_ret)
    xmeta = np.zeros((128, D), f32); xmeta[:NM] = A(meta)
    gcols = np.concatenate([A(g_mix)[0].reshape(8, 128).T, A(g_ffn)[0].reshape(8, 128).T], axis=1)
    gqk = np.concatenate([np.broadcast_to(A(g_q)[0][None, :], (128, 64)), np.broadcast_to(A(g_k)[0][None, :], (128, 64))], axis=1)
    gsub = A(g_sub)[0].reshape(128, 1)
    lam4 = np.concatenate([A(lam_q1)[0], A(lam_k1)[0], A(lam_q2)[0], A(lam_k2)[0]])[None, :]
    lam4 = np.broadcast_to(lam4, (128, 256))
    shared = dict(xmeta=xmeta, w_in=A(w_in)[0], w_out=A(w_out)[0], w_gate=A(w_gate)[0], w_up=A(w_up)[0], w_down=A(w_down)[0],
                  gcols=A(gcols), gqk=A(gqk), gsub=A(gsub), lam4=A(lam4),
                  rope=cst["rope"], small=cst["small"], dmask=cst["dmask"], cbf=cst["cbf"])
    in_maps = []
    for c in range(NCORES):
        m = dict(shared)
        m["xp"] = x_prompt[2 * c:2 * c + 2]
        m["xs"] = x_sample[2 * c:2 * c + 2].reshape(128, D)
        m["ck"] = cache_k[0, 2 * c:2 * c + 2].reshape(2, LCACHE, 512)
        m["cv"] = cache_v[0, 2 * c:2 * c + 2].reshape(2, LCACHE, 512)
        m["st_in"] = state_ret[0, 2 * c:2 * c + 2]
        in_maps.append(m)
    res = run_bass_kernel_spmd(nc, in_maps, core_ids=list(range(NCORES)))
    R = res.results
    cat = lambda k: np.concatenate([np.asarray(r[k]) for r in R], axis=0)
    y_prompt = cat("yp").reshape(16, SEQ, D)
    y_sample = np.concatenate([np.asarray(r["ys"]).reshape(2, DEC, D) for r in R], axis=0)
    k_prompt = cat("kp").reshape(1, 16, NM + SEQ, 4, 128)
    v_prompt = cat("vp").reshape(1, 16, NM + SEQ, 4, 128)
    state_prompt = cat("stp").reshape(1, 16, 4, 64, 128)
    k_sample = np.concatenate([np.asarray(r["ks"]).reshape(2, DEC, 4, 128) for r in R], axis=0)[None]
    v_sample = np.concatenate([np.asarray(r["vs"]).reshape(2, DEC, 4, 128) for r in R], axis=0)[None]
    state_sample = cat("sts").reshape(1, 16, 4, 64, 128)
    return tuple(np.ascontiguousarray(a, dtype=f32) for a in
                 (y_prompt, y_sample, k_prompt, v_prompt, state_prompt, k_sample, v_sample, state_sample))
```
